# Optimizing a Trainium2 kernel written in Bass

```python
import jax, jax.numpy as jnp
from jax import lax
import numpy as np

D_MODEL = 1024
BATCH = 8
SEQ = 2048
DEPTH = 1

CHUNK = 64
MEM_LEN = 256
CONV_DIM = D_MODEL
CONV_WIDTH = 3
SB_HEADS = 16
SB_HEAD_DIM = 64
SB_DIM = SB_HEADS * SB_HEAD_DIM
SB_BLOCK = 128
X_HEADS = 4
X_HEAD_DIM = 256
X_DIM = X_HEADS * X_HEAD_DIM
N_BRANCH = 3
D_FF = 4 * D_MODEL
EPS = 1e-6

IN_SPLITS = [CONV_DIM, CONV_DIM, CONV_DIM, SB_DIM, SB_DIM, SB_DIM, X_DIM, N_BRANCH * D_MODEL]
IN_COLS = int(sum(IN_SPLITS))

kernel_name = "hybrid_conv_stickbreaking_memxattn_block"


def rms_norm(x, g):
    xf = x.astype(jnp.float32)
    y = xf * lax.rsqrt(jnp.mean(xf * xf, axis=-1, keepdims=True) + EPS)
    return (y * g.astype(jnp.float32)).astype(x.dtype)


def causal_depthwise_conv(u, w):
    c = u.shape[-1]
    return lax.conv_general_dilated(
        u, w.astype(u.dtype)[:, None, :], window_strides=(1,),
        padding=[(CONV_WIDTH - 1, 0)],
        dimension_numbers=("NWC", "WIO", "NWC"),
        feature_group_count=c)


def stick_breaking_attention(q, k, v):
    seq = q.shape[2]
    scale = q.shape[-1] ** -0.5
    outs = []
    for start in range(0, seq, SB_BLOCK):
        end = start + SB_BLOCK
        qb = q[:, :, start:end]
        kb = k[:, :, :end]
        vb = v[:, :, :end]
        z = jnp.einsum("bhqd,bhkd->bhqk", qb, kb).astype(jnp.float32) * scale
        t_idx = start + jnp.arange(SB_BLOCK)[:, None]
        s_idx = jnp.arange(end)[None, :]
        past = s_idx < t_idx
        log_beta = jax.nn.log_sigmoid(z)
        log_1mb = jnp.where(past, jax.nn.log_sigmoid(-z), 0.0)
        between = lax.cumsum(log_1mb, axis=3, reverse=True) - log_1mb
        a = jnp.where(past, jnp.exp(log_beta + between), 0.0)
        outs.append(jnp.einsum("bhqk,bhkd->bhqd", a.astype(vb.dtype), vb))
    return jnp.concatenate(outs, axis=2)


def memory_cross_attention(xq, mem_n, w_mem_kv, q_norm_g, k_norm_g):
    b, s, _ = xq.shape
    m = mem_n.shape[1]
    q = xq.reshape(b, s, X_HEADS, X_HEAD_DIM)
    kv = mem_n @ w_mem_kv
    k, v = jnp.split(kv, 2, axis=-1)
    k = k.reshape(b, m, X_HEADS, X_HEAD_DIM)
    v = v.reshape(b, m, X_HEADS, X_HEAD_DIM)
    q = rms_norm(q, q_norm_g)
    k = rms_norm(k, k_norm_g)
    scores = jnp.einsum("bqhd,bkhd->bhqk", q, k).astype(jnp.float32) * (X_HEAD_DIM ** -0.5)
    p = jax.nn.softmax(scores, axis=-1).astype(v.dtype)
    o = jnp.einsum("bhqk,bkhd->bqhd", p, v)
    return o.reshape(b, s, X_DIM)


def setup_inputs(seed: int = 0) -> dict:
    key = jax.random.key(seed)
    ks = jax.random.split(key, 20)
    f32 = jnp.float32

    def nrm(k, shape, fan_in):
        return jax.random.normal(k, shape, f32) * (fan_in ** -0.5)

    def gain(k, shape):
        return 1.0 + 0.02 * jax.random.normal(k, shape, f32)

    L = DEPTH
    return {
        "x": jax.random.normal(ks[0], (BATCH, SEQ, D_MODEL), f32),
        "mem": jax.random.normal(ks[1], (BATCH, MEM_LEN, D_MODEL), f32),
        "g_mix": gain(ks[2], (L, D_MODEL)),
        "g_mem": gain(ks[3], (L, D_MODEL)),
        "w_in": nrm(ks[4], (L, D_MODEL, IN_COLS), D_MODEL),
        "conv_w": nrm(ks[5], (L, CONV_WIDTH, CONV_DIM), CONV_WIDTH),
        "w_conv_out": nrm(ks[6], (L, CONV_DIM, D_MODEL), CONV_DIM),
        "w_sb_out": nrm(ks[7], (L, SB_DIM, D_MODEL), SB_DIM),
        "q_norm_g": gain(ks[8], (L, X_HEAD_DIM)),
        "k_norm_g": gain(ks[9], (L, X_HEAD_DIM)),
        "w_mem_kv": nrm(ks[10], (L, D_MODEL, 2 * X_DIM), D_MODEL),
        "w_x_out": nrm(ks[11], (L, X_DIM, D_MODEL), X_DIM),
        "w_out": nrm(ks[12], (L, D_MODEL, D_MODEL), D_MODEL),
        "g_mlp": gain(ks[13], (L, D_MODEL)),
        "w_up": nrm(ks[14], (L, D_MODEL, D_FF), D_MODEL),
        "w_down": nrm(ks[15], (L, D_FF, D_MODEL), D_FF),
    }


def reference(x, mem, g_mix, g_mem, w_in, conv_w, w_conv_out, w_sb_out, q_norm_g,
              k_norm_g, w_mem_kv, w_x_out, w_out, g_mlp, w_up, w_down):
    b, s, d = x.shape
    split_idx = list(np.cumsum(IN_SPLITS)[:-1])
    for l in range(DEPTH):
        h = rms_norm(x, g_mix[l])
        proj = h @ w_in[l]
        c_h, c_b, c_c, sq, sk, sv, xq, gate_pre = jnp.split(proj, split_idx, axis=-1)

        y_conv = (c_b * causal_depthwise_conv(c_c * c_h, conv_w[l])) @ w_conv_out[l]

        to_heads = lambda t: t.reshape(b, s, SB_HEADS, SB_HEAD_DIM).transpose(0, 2, 1, 3)
        o_sb = stick_breaking_attention(to_heads(sq), to_heads(sk), to_heads(sv))
        y_sb = o_sb.transpose(0, 2, 1, 3).reshape(b, s, SB_DIM) @ w_sb_out[l]

        mem_n = rms_norm(mem, g_mem[l])
        y_x = memory_cross_attention(xq, mem_n, w_mem_kv[l], q_norm_g[l], k_norm_g[l]) @ w_x_out[l]

        gates = jax.nn.sigmoid(gate_pre.astype(jnp.float32)).astype(x.dtype)
        gates = gates.reshape(b, s, N_BRANCH, d)
        merged = gates[:, :, 0] * y_conv + gates[:, :, 1] * y_sb + gates[:, :, 2] * y_x
        x = x + merged @ w_out[l]

        h2 = rms_norm(x, g_mlp[l])
        x = x + jnp.square(jax.nn.relu(h2 @ w_up[l])) @ w_down[l]
    return x
```

```python
import numpy as np
import ml_dtypes
from contextlib import ExitStack
import concourse.bass as bass
import concourse.mybir as mybir
from concourse.bass_utils import run_bass_kernel_spmd

F32 = mybir.dt.float32
BF = mybir.dt.bfloat16
AF = mybir.ActivationFunctionType
ALU = mybir.AluOpType

ENGS = ("pe", "act", "dve", "pool", "sp")

S = 2048
D = 1024
TT = 512
NT = S // TT
MEM = 256
DFF = 4096
INC = 10240
EPS = 1e-6
NEG = -30000.0


class Op:
    __slots__ = ("eng", "fn", "deps", "signal", "sval", "is_dma", "dsem", "dval")

    def __init__(self, eng, fn, is_dma):
        self.eng = eng
        self.fn = fn
        self.deps = []
        self.signal = False
        self.sval = 0
        self.is_dma = is_dma
        self.dsem = None
        self.dval = 0


class Prog:
    def __init__(self, n_dma_sems=None, self_sync=True):
        self.ops = {e: [] for e in ENGS}
        self.last_w = {}
        self.readers = {}
        self.n_dma_sems = n_dma_sems or {"sp": 8, "pool": 24, "act": 4}
        self.dma_count = {}
        self.dma_rr = {e: 0 for e in ENGS}
        self.dma_last = {}
        self.self_sync = self_sync

    def op(self, eng, fn, reads=(), writes=(), dma=False):
        o = Op(eng, fn, dma)
        deps = []
        for t in reads:
            w = self.last_w.get(t)
            if w is not None:
                deps.append(w)
        for t in writes:
            w = self.last_w.get(t)
            if w is not None:
                deps.append(w)
            deps.extend(self.readers.get(t, ()))
        if dma:
            k = self.dma_rr[eng] % self.n_dma_sems[eng]
            self.dma_rr[eng] += 1
            key = (eng, k)
            self.dma_count[key] = self.dma_count.get(key, 0) + 1
            o.dsem = key
            o.dval = 16 * self.dma_count[key]
            prev = self.dma_last.get(key)
            if prev is not None:
                deps.append(prev)
            self.dma_last[key] = o
        seen = set()
        for d in deps:
            if id(d) in seen:
                continue
            seen.add(id(d))
            if (not d.is_dma) and d.eng == eng:
                if eng == "pe" or not self.self_sync:
                    continue
            o.deps.append(d)
            if not d.is_dma:
                d.signal = True
        for t in reads:
            self.readers.setdefault(t, []).append(o)
        for t in writes:
            self.last_w[t] = o
            self.readers[t] = []
        self.ops[eng].append(o)
        return o

    def emit(self, block_engines, sems, dma_sems, final_waits):
        for e in ENGS:
            c = 0
            for o in self.ops[e]:
                if o.signal and not o.is_dma:
                    c += 1
                    o.sval = c

        def dep_key(d):
            if d.is_dma:
                return ("d",) + d.dsem, d.dval
            return ("e", d.eng), d.sval

        def sem_of(k):
            return dma_sems[k[1:]] if k[0] == "d" else sems[k[1]]

        def run(e, eng):
            known = {}
            for o in self.ops[e]:
                need = {}
                for d in o.deps:
                    k, v = dep_key(d)
                    if v > need.get(k, 0):
                        need[k] = v
                for k, v in need.items():
                    if known.get(k, 0) >= v:
                        continue
                    known[k] = v
                    eng.wait_ge(sem_of(k), v)
                ins = o.fn(eng)
                if o.is_dma:
                    ins.then_inc(dma_sems[o.dsem], 16)
                elif o.signal:
                    ins.then_inc(sems[e], 1)
            for d in final_waits.get(e, ()):
                k, v = dep_key(d)
                eng.wait_ge(sem_of(k), v)

        for e in ENGS:
            block_engines[e](lambda eng, e=e: run(e, eng))


def build(ntiles=NT, debug=(), lvl=9, skip=()):
    nc = bass.Bass("TRN2", target_bir_lowering=False)

    def din(name, shape, dt=F32):
        return nc.dram_tensor(name, list(shape), dt, kind="ExternalInput").ap()

    x_d = din("x", [S, D])
    mem_d = din("mem", [MEM, D])
    w_in_d = din("w_in", [D, INC])
    w_co_d = din("w_conv_out", [D, D])
    w_so_d = din("w_sb_out", [D, D])
    w_xo_d = din("w_x_out", [D, D])
    w_o_d = din("w_out", [D, D])
    w_kv_d = din("w_mem_kv", [D, 2 * D])
    w_up_d = din("w_up", [D, DFF])
    w_dn_d = din("w_down", [DFF, D])
    vec_d = din("vec", [128, 52])
    cst_d = din("cst", [128, 5, 128], BF)
    out_d = nc.dram_tensor("out", [S, D], F32, kind="ExternalOutput").ap()

    def dint(name, shape):
        return nc.dram_tensor(name, list(shape), BF, kind="Internal").ap()

    b_in = dint("b_in", [D, INC])
    b_co = dint("b_co", [D, D])
    b_so = dint("b_so", [D, D])
    b_xo = dint("b_xo", [D, D])
    b_o = dint("b_o", [D, D])
    b_kv = dint("b_kv", [D, 2 * D])
    b_up = dint("b_up", [D, DFF])
    b_dn = dint("b_dn", [DFF, D])

    dbg_outs = {}
    P = Prog()
    with ExitStack() as es:
        def sb(name, shape, dt):
            return es.enter_context(nc.sbuf_tensor(name, list(shape), dt))

        xt = sb("xt", [128, 4, D], F32)
        hT = sb("hT", [128, 8, TT], BF)
        big = sb("big", [128, 32, TT], BF)
        osbT = sb("osbT", [128, 8, TT], BF)
        kT = sb("kT", [128, 8, S], BF)
        vA = sb("vA", [128, 16, D], BF)
        knT = sb("knT", [128, 8, MEM], BF)
        vmem = sb("vmem", [128, 2, D], BF)
        NRING = 4
        wring = [sb(f"ws{i}", [128, 8, 512], BF) for i in range(NRING)]
        junk = sb("junk", [128, D], BF)
        hb = [sb(f"hb{i}", [128, D], BF) for i in range(2)]
        NTF = 6
        tf = [sb(f"tf{i}", [128, 512], F32) for i in range(NTF)]
        NTB = 8
        tb = [sb(f"tb{i}", [128, 512], BF) for i in range(NTB)]
        ubuf = [sb(f"ub{i}", [128, 516], F32) for i in range(2)]
        ucar = sb("ucar", [128, 8, 2], F32)
        cst = sb("cst_sb", [128, 5, 128], BF)
        vec = sb("vecs", [128, 52], F32)
        gqk = sb("gqk", [128, 2], F32)
        NSTAT = 8
        stat = sb("stat", [128, NSTAT, 4], F32)
        banks = [es.enter_context(nc.psum_tensor(f"ps{i}", [128, 512], F32)) for i in range(8)]
        sems = {e: es.enter_context(nc.semaphore(f"s_{e}")) for e in ENGS}
        dsems = {}
        for e in ("sp", "pool", "act"):
            for k in range(P.n_dma_sems[e]):
                dsems[(e, k)] = es.enter_context(nc.semaphore(f"d_{e}{k}"))
        block = es.enter_context(nc.Block())

        ident = cst[:, 0, :]
        negU = cst[:, 1, :]
        negones = cst[:, 2, :]
        ones = cst[:, 3, :]
        negmask = cst[:, 4, :]
        gmix = vec[:, 0:8]
        gmem = vec[:, 8:16]
        gmlp = vec[:, 16:24]
        convw = vec[:, 24:48].rearrange("p (c k) -> p c k", k=3)
        qg = vec[:, 48:50]
        kg = vec[:, 50:52]

        qT = big[:, 0:8, :]
        convT = big[:, 8:16, :]
        xqnT = big[:, 16:24, :]
        mergedT = big[:, 16:24, :]
        oxT = big[:, 24:32, :]
        uT = big

        final_ops = []

        state = {"bank": 0, "ring": 0, "stat": 0, "hb": 0, "tf": 0, "tb": 0, "ub": 0, "ev": 0}

        def nb():
            i = state["bank"] % 8
            state["bank"] += 1
            return i

        def ntf():
            i = state["tf"] % NTF
            state["tf"] += 1
            return i

        def ntb():
            i = state["tb"] % NTB
            state["tb"] += 1
            return i

        def dbg(name, ap, shape, reads):
            if name not in debug:
                return
            dt = ap.dtype
            t = nc.dram_tensor("dbg_" + name, list(shape), dt, kind="ExternalOutput").ap()
            dbg_outs[name] = t
            o = P.op("pool", C("dma_start", out=t, in_=ap), reads=reads, dma=True)
            final_ops.append(o)

        def C(meth, *a, **kw):
            return lambda e: getattr(e, meth)(*a, **kw)

        def act(fn, reads, writes):
            return P.op("act", fn, reads, writes)

        def dve(fn, reads, writes):
            return P.op("dve", fn, reads, writes)

        def pool(fn, reads, writes):
            return P.op("pool", fn, reads, writes)

        def mm(out, lhsT, rhs, start, stop, reads, writes, skip=False):
            return P.op("pe", C("matmul", out, lhsT=lhsT, rhs=rhs, start=start, stop=stop, skip_group_check=skip),
                        reads, writes)

        def evac(dst, src, reads, writes, scale=None):
            state["ev"] += 1
            if state["ev"] % 2 == 0:
                if scale is None:
                    act(C("activation", out=dst, in_=src, func=AF.Copy), reads, writes)
                else:
                    act(C("activation", out=dst, in_=src, func=AF.Copy, scale=scale), reads, writes)
            else:
                if scale is None:
                    dve(C("tensor_copy", out=dst, in_=src), reads, writes)
                else:
                    dve(C("tensor_scalar", out=dst, in0=src, scalar1=scale, scalar2=None, op0=ALU.mult),
                        reads, writes)

        def stream(wap, r0, c0, segtok, ncols=512):
            if ncols == 512:
                if state["ring"] % 2:
                    state["ring"] += 1
                h = state["ring"] % (2 * NRING)
                state["ring"] += 2
                dst = wring[h // 2][:, :, :]
                toks = [("ws", h), ("ws", h + 1)]
            else:
                h = state["ring"] % (2 * NRING)
                state["ring"] += 1
                dst = wring[h // 2][:, :, (h % 2) * 256:(h % 2) * 256 + 256]
                toks = [("ws", h)]
            src = wap[r0:r0 + 1024, c0:c0 + ncols].rearrange("(kc p) n -> p kc n", p=128)
            P.op("sp", C("dma_start", out=dst, in_=src), reads=[segtok], writes=toks, dma=True)
            return dst, toks

        def mm8_feat(bank, slot, slot_tok, cc, rhsT, rhs_toks, n=TT):
            for kc in range(8):
                mm(banks[bank][:, 0:n], slot[:, kc, cc * 128:(cc + 1) * 128], rhsT[:, kc, 0:n],
                   kc == 0, kc == 7, slot_tok + rhs_toks, [("ps", bank)])

        def mm8_tok(bank, lhsT_buf, lhs_toks, b, slot, slot_tok, n=512):
            for kc in range(8):
                mm(banks[bank][:, 0:n], lhsT_buf[:, kc, b * 128:(b + 1) * 128], slot[:, kc, 0:n],
                   kc == 0, kc == 7, slot_tok + lhs_toks, [("ps", bank)])

        HT_TOKS = [("hT", b) for b in range(4)]

        def rstd_small(src_ap, n_inv, reads):
            si = state["stat"] % NSTAT
            state["stat"] += 1
            ss = stat[:, si, 0:1]
            lv = stat[:, si, 1:2]
            rs = stat[:, si, 2:3]
            tk = ("stat", si)
            act(C("activation", out=junk[:], in_=src_ap, func=AF.Square, accum_out=ss),
                reads, ["junk", tk])
            act(C("activation", out=lv, in_=ss, func=AF.Ln, scale=n_inv, bias=EPS), [tk], [tk])
            act(C("activation", out=rs, in_=lv, func=AF.Exp, scale=-0.5), [tk], [tk])
            return rs, tk

        def norm_transpose(src_ap, src_toks, gv, dstT, col0, dst_tok):
            rs, tk = rstd_small(src_ap, 1.0 / D, src_toks)
            hi = state["hb"] % 2
            state["hb"] += 1
            h = hb[hi]
            act(C("activation", out=h[:], in_=src_ap, func=AF.Copy, scale=rs), src_toks + [tk], [("hb", hi)])
            bk = nb()
            pb = banks[bk][:].bitcast(BF)
            for c in range(8):
                P.op("pe", C("transpose", out=pb[:, c * 128:(c + 1) * 128],
                                                      in_=h[:, c * 128:(c + 1) * 128], identity=ident),
                     [("hb", hi), "cst"], [("ps", bk)])
            dve(C("tensor_tensor", out=dstT[:, :, col0:col0 + 128],
                                          in0=pb.rearrange("p (c t) -> p c t", c=8),
                                          in1=gv.unsqueeze(2).to_broadcast([128, 8, 128]), op=ALU.mult),
                [("ps", bk), "vec"], [dst_tok])

        def rstd_bcast(bankC, n_inv, n):
            ti = ntf()
            t = tf[ti]
            act(C("activation", out=t[:, 0:n], in_=banks[bankC][:, 0:n], func=AF.Ln, scale=n_inv, bias=EPS),
                [("ps", bankC)], [("tf", ti)])
            act(C("activation", out=t[:, 0:n], in_=t[:, 0:n], func=AF.Exp, scale=-0.5),
                [("tf", ti)], [("tf", ti)])
            return t, ("tf", ti)

        P.op("sp", C("dma_start", out=cst[:], in_=cst_d), writes=["cst"], dma=True)
        P.op("sp", C("dma_start", out=vec[:], in_=vec_d), writes=["vec"], dma=True)
        for mb in range(2):
            P.op("sp", C("dma_start", out=xt[:, mb, :], in_=mem_d[mb * 128:(mb + 1) * 128, :]),
                 writes=[("xt", mb)], dma=True)

        def cast(dst, src, tok):
            P.op("pool", C("dma_start", out=dst, in_=src), writes=[tok], dma=True)

        cast(b_kv[:, :], w_kv_d[:, :], ("w", "kv", 0))
        for sgm in (1, 2, 0, 3, 4):
            cast(b_in[:, sgm * 2048:(sgm + 1) * 2048], w_in_d[:, sgm * 2048:(sgm + 1) * 2048], ("w", "in", sgm))
        cast(b_co[:, :], w_co_d[:, :], ("w", "co", 0))
        cast(b_so[:, :], w_so_d[:, :], ("w", "so", 0))
        cast(b_xo[:, :], w_xo_d[:, :], ("w", "xo", 0))
        cast(b_o[:, :], w_o_d[:, :], ("w", "o", 0))
        for sgm in range(2):
            cast(b_up[:, sgm * 2048:(sgm + 1) * 2048], w_up_d[:, sgm * 2048:(sgm + 1) * 2048], ("w", "up", sgm))
        for rg in range(4):
            cast(b_dn[rg * 1024:(rg + 1) * 1024, :], w_dn_d[rg * 1024:(rg + 1) * 1024, :], ("w", "dn", rg))

        pool(C("memset", ucar[:], 0.0), [], ["ucar"])
        dve(C("scalar_tensor_tensor", out=gqk[:], in0=qg, scalar=1.0 / 16.0, in1=kg,
                                             op0=ALU.mult, op1=ALU.mult), ["vec"], ["gqk"])

        memT = osbT[:, :, 0:MEM]
        OSB_TOKS = [("osb", c) for c in range(8)]
        for mb in range(2 if lvl >= 1 else 0):
            norm_transpose(xt[:, mb, :], [("xt", mb)], gmem, memT, mb * 128, ("memT", mb))
        MEMT_TOKS = [("memT", 0), ("memT", 1)] + OSB_TOKS
        for blk in range(2 if lvl >= 1 else 0):
            slot, stok = stream(b_kv, 0, blk * 512, ("w", "kv", 0))
            for hl in range(2):
                hx = blk * 2 + hl
                bk = [nb(), nb()]
                sq = []
                for dc in range(2):
                    mm8_feat(bk[dc], slot, stok, hl * 2 + dc, memT, MEMT_TOKS, n=MEM)
                    ti = ntb()
                    act(C("activation", out=tb[ti][:, 0:MEM], in_=banks[bk[dc]][:, 0:MEM],
                                                             func=AF.Square), [("ps", bk[dc])], [("tb", ti)])
                    sq.append(ti)
                bC = nb()
                for dc in range(2):
                    mm(banks[bC][:, 0:MEM], ones, tb[sq[dc]][:, 0:MEM], dc == 0, dc == 1,
                       ["cst", ("tb", sq[dc])], [("ps", bC)])
                rq, rtok = rstd_bcast(bC, 1.0 / 256, MEM)
                for dc in range(2):
                    dve(C("scalar_tensor_tensor", out=knT[:, 2 * hx + dc, :], in0=banks[bk[dc]][:, 0:MEM],
                                                                scalar=gqk[:, dc:dc + 1], in1=rq[:, 0:MEM],
                                                                op0=ALU.mult, op1=ALU.mult),
                        [("ps", bk[dc]), rtok, "gqk"], [("knT", 2 * hx + dc)])
        for half in range(2 if lvl >= 1 else 0):
            slot, stok = stream(b_kv, 0, 1024 + half * 512, ("w", "kv", 0))
            for mc in range(2):
                bk = nb()
                mm8_tok(bk, memT, MEMT_TOKS, mc, slot, stok)
                evac(vmem[:, mc, half * 512:(half + 1) * 512], banks[bk][:], [("ps", bk)], [("vmem", mc, half)])
        KN_TOKS = [("knT", c) for c in range(8)]
        VM_TOKS = [("vmem", mc, h) for mc in range(2) for h in range(2)]
        dbg("knT", knT[:], [128, 8, MEM], KN_TOKS)
        dbg("vmem", vmem[:], [128, 2, D], VM_TOKS)

        for T in range(ntiles):
            t0 = T * TT
            for b in range(4):
                P.op("act", C("dma_start", out=xt[:, b, :], in_=x_d[t0 + b * 128:t0 + (b + 1) * 128, :]),
                     writes=[("xt", b)], dma=True)
            for b in range(4 if lvl >= 2 else 0):
                norm_transpose(xt[:, b, :], [("xt", b)], gmix, hT, b * 128, ("hT", b))
            if T == 0:
                dbg("hT", hT[:], [128, 8, TT], HT_TOKS)

            for blk in range(2 if lvl >= 2 else 0):
                slot, stok = stream(b_in, 0, 3072 + blk * 512, ("w", "in", 1))
                for cc in range(4):
                    c = blk * 4 + cc
                    bk = nb()
                    mm8_feat(bk, slot, stok, cc, hT, HT_TOKS)
                    evac(qT[:, c, :], banks[bk][:], [("ps", bk)], [("big", c)], scale=0.125)
            for blk in range(2 if lvl >= 2 else 0):
                slot, stok = stream(b_in, 0, 4096 + blk * 512, ("w", "in", 2))
                for cc in range(4):
                    c = blk * 4 + cc
                    bk = nb()
                    mm8_feat(bk, slot, stok, cc, hT, HT_TOKS)
                    evac(kT[:, c, t0:t0 + TT], banks[bk][:], [("ps", bk)], [("kT", c, T)])
            for half in range(2 if lvl >= 2 else 0):
                slot, stok = stream(b_in, 0, 5120 + half * 512, ("w", "in", 2))
                for b in range(4):
                    bk = nb()
                    mm8_tok(bk, hT, HT_TOKS, b, slot, stok)
                    evac(vA[:, T * 4 + b, half * 512:(half + 1) * 512], banks[bk][:], [("ps", bk)],
                         [("vA", T * 4 + b, half)])
            if T == 0:
                dbg("qT", qT, [128, 8, TT], [("big", c) for c in range(8)])
                dbg("kT0", kT[:, :, 0:TT], [128, 8, TT], [("kT", c, 0) for c in range(8)])
                dbg("v0", vA[:, 0:4, :], [128, 4, D], [("vA", b, h) for b in range(4) for h in range(2)])

            for j in range(2 if lvl >= 3 else 0):
                sl_h, tk_h = stream(b_in, 0, j * 512, ("w", "in", 0))
                sl_c, tk_c = stream(b_in, 0, 2048 + j * 512, ("w", "in", 1))
                sl_b, tk_b = stream(b_in, 0, 1024 + j * 512, ("w", "in", 0))
                for cc in range(4):
                    c = j * 4 + cc
                    bh = nb()
                    mm8_feat(bh, sl_h, tk_h, cc, hT, HT_TOKS)
                    thi = ntf()
                    act(C("activation", out=tf[thi][:], in_=banks[bh][:], func=AF.Copy),
                        [("ps", bh)], [("tf", thi)])
                    bc = nb()
                    mm8_feat(bc, sl_c, tk_c, cc, hT, HT_TOKS)
                    ui = state["ub"] % 2
                    state["ub"] += 1
                    ub = ubuf[ui]
                    utok = ("ub", ui)
                    pool(C("tensor_copy", out=ub[:, 0:2], in_=ucar[:, c, :]), ["ucar"], [utok])
                    dve(C("tensor_tensor", out=ub[:, 2:514], in0=tf[thi][:],
                                                                         in1=banks[bc][:], op=ALU.mult),
                        [("tf", thi), ("ps", bc)], [utok])
                    pool(C("tensor_copy", out=ucar[:, c, :], in_=ub[:, 512:514]), [utok], ["ucar"])
                    ai = ntf()
                    acc = tf[ai]
                    atok = ("tf", ai)
                    dve(C("tensor_scalar", out=acc[:], in0=ub[:, 2:514],
                                                                       scalar1=convw[:, c, 2:3], scalar2=None,
                                                                       op0=ALU.mult), [utok, "vec"], [atok])
                    dve(C("scalar_tensor_tensor", out=acc[:], in0=ub[:, 1:513],
                                                                              scalar=convw[:, c, 1:2], in1=acc[:],
                                                                              op0=ALU.mult, op1=ALU.add),
                        [utok, "vec", atok], [atok])
                    dve(C("scalar_tensor_tensor", out=acc[:], in0=ub[:, 0:512],
                                                                              scalar=convw[:, c, 0:1], in1=acc[:],
                                                                              op0=ALU.mult, op1=ALU.add),
                        [utok, "vec", atok], [atok])
                    bb = nb()
                    mm8_feat(bb, sl_b, tk_b, cc, hT, HT_TOKS)
                    dve(C("tensor_tensor", out=convT[:, c, :], in0=acc[:], in1=banks[bb][:],
                                                                       op=ALU.mult),
                        [atok, ("ps", bb)], [("big", 8 + c)])
            if T == 0:
                dbg("convT", convT, [128, 8, TT], [("big", 8 + c) for c in range(8)])

            for blk in range(2 if lvl >= 4 else 0):
                slot, stok = stream(b_in, 0, 6144 + blk * 512, ("w", "in", 3))
                for hl in range(2):
                    hx = blk * 2 + hl
                    bk = [nb(), nb()]
                    sq = []
                    for dc in range(2):
                        mm8_feat(bk[dc], slot, stok, hl * 2 + dc, hT, HT_TOKS)
                        ti = ntb()
                        act(C("activation", out=tb[ti][:], in_=banks[bk[dc]][:],
                                                                        func=AF.Square), [("ps", bk[dc])], [("tb", ti)])
                        sq.append(ti)
                    bC = nb()
                    for dc in range(2):
                        mm(banks[bC][:], ones, tb[sq[dc]][:], dc == 0, dc == 1, ["cst", ("tb", sq[dc])], [("ps", bC)])
                    rq, rtok = rstd_bcast(bC, 1.0 / 256, TT)
                    for dc in range(2):
                        dve(C("tensor_tensor", out=xqnT[:, 2 * hx + dc, :],
                                                                                  in0=rq[:], in1=banks[bk[dc]][:],
                                                                                  op=ALU.mult),
                            [("ps", bk[dc]), rtok], [("big", 16 + 2 * hx + dc)])
            if T == 0:
                dbg("xqnT", xqnT, [128, 8, TT], [("big", 16 + c) for c in range(8)])
            for hx in range(4 if lvl >= 4 else 0):
                pts = []
                for mc in range(2):
                    bS = nb()
                    for dc in range(2):
                        mm(banks[bS][:], knT[:, 2 * hx + dc, mc * 128:(mc + 1) * 128], xqnT[:, 2 * hx + dc, :],
                           dc == 0, dc == 1, KN_TOKS + [("big", 16 + 2 * hx + dc)], [("ps", bS)])
                    ti = ntb()
                    act(C("activation", out=tb[ti][:], in_=banks[bS][:], func=AF.Exp),
                        [("ps", bS)], [("tb", ti)])
                    pts.append(ti)
                bD = nb()
                for mc in range(2):
                    mm(banks[bD][:], ones, tb[pts[mc]][:], mc == 0, mc == 1, ["cst", ("tb", pts[mc])], [("ps", bD)])
                ri = ntf()
                dve(C("reciprocal", out=tf[ri][:], in_=banks[bD][:]), [("ps", bD)], [("tf", ri)])
                for dc in range(2):
                    bO = nb()
                    for mc in range(2):
                        mm(banks[bO][:], vmem[:, mc, hx * 256 + dc * 128: hx * 256 + (dc + 1) * 128], tb[pts[mc]][:],
                           mc == 0, mc == 1, VM_TOKS + [("tb", pts[mc])], [("ps", bO)])
                    dve(C("tensor_tensor", out=oxT[:, 2 * hx + dc, :], in0=tf[ri][:],
                                                                              in1=banks[bO][:], op=ALU.mult),
                        [("tf", ri), ("ps", bO)], [("big", 24 + 2 * hx + dc)])
            if T == 0:
                dbg("oxT", oxT, [128, 8, TT], [("big", 24 + c) for c in range(8)])

            nk = 4 * T + 4
            units = []
            for pr in range(8 if (lvl >= 5 and "sb" not in skip) else 0):
                for kb in range(nk - 1, -1, -1):
                    for hh in range(2):
                        units.append({"pr": pr, "hh": hh, "kb": kb, "i": len(units)})
            ZB = [0, 1, 2, 3, 4]
            OB = [5, 6]
            EB = [0, 1]
            SPB = [0, 1, 2]
            AB = [3, 4, 5]
            RB = [6, 7]

            def kv_toks(u):
                pr, kb = u["pr"], u["kb"]
                return ("kT", pr, kb // 4), [("vA", kb, (2 * pr) // 8)]

            def s0(u):
                pr, hh, kb, i = u["pr"], u["hh"], u["kb"], u["i"]
                c0 = max(0, kb - 4 * T) * 128
                u["c0"] = c0
                zb = ZB[i % len(ZB)]
                u["zb"] = zb
                diag = kb >= 4 * T
                p0 = 64 * hh
                ktok, _ = kv_toks(u)
                mm(banks[zb][:, c0:512], kT[p0:p0 + 64, pr, kb * 128:(kb + 1) * 128], qT[p0:p0 + 64, pr, c0:512],
                   True, True, [ktok, ("big", pr)], [("ps", zb)])
                if diag:
                    mm(banks[zb][:, c0:c0 + 128], ident, negmask, False, True, ["cst"], [("ps", zb)], skip=True)

            def s1(u):
                i, c0, zb = u["i"], u["c0"], u["zb"]
                eb = EB[i % len(EB)]
                u["eb"] = eb
                act(C("activation", out=tf[eb][:, c0:512], in_=banks[zb][:, c0:512], func=AF.Exp),
                    [("ps", zb)], [("tf", eb)])

            def s2(u):
                i, c0, eb = u["i"], u["c0"], u["eb"]
                sp = SPB[i % len(SPB)]
                u["sp"] = sp
                act(C("activation", out=tb[sp][:, c0:512], in_=tf[eb][:, c0:512], func=AF.Ln, bias=1.0, scale=1.0),
                    [("tf", eb)], [("tb", sp)])

            def s3(u):
                hh, kb, c0, zb, sp = u["hh"], u["kb"], u["c0"], u["zb"], u["sp"]
                rb = RB[hh]
                has_r = kb < nk - 1
                mm(banks[zb][:, c0:512], negU, tb[sp][:, c0:512], False, not has_r, ["cst", ("tb", sp)], [("ps", zb)],
                   skip=True)
                if has_r:
                    mm(banks[zb][:, c0:512], negones, tb[rb][:, c0:512], False, True, ["cst", ("tb", rb)], [("ps", zb)],
                       skip=True)
                if kb == nk - 1:
                    pool(C("memset", tb[rb][:], 0.0), [], [("tb", rb)])
                if kb > 0:
                    pool(C("tensor_tensor", out=tb[rb][:, c0:512], in0=tb[rb][:, c0:512], in1=tb[sp][:, c0:512],
                                                   op=ALU.add), [("tb", rb), ("tb", sp)], [("tb", rb)])

            def s4(u):
                i, c0, zb = u["i"], u["c0"], u["zb"]
                ab = AB[i % len(AB)]
                u["ab"] = ab
                act(C("activation", out=tb[ab][:, c0:512], in_=banks[zb][:, c0:512], func=AF.Exp),
                    [("ps", zb)], [("tb", ab)])

            def s5(u):
                pr, hh, kb, c0, ab = u["pr"], u["hh"], u["kb"], u["c0"], u["ab"]
                ob = OB[pr % 2]
                hd = 2 * pr + hh
                _, vtoks = kv_toks(u)
                mm(banks[ob][64 * hh:64 * hh + 64, c0:512], vA[:, kb, hd * 64:(hd + 1) * 64], tb[ab][:, c0:512],
                   kb == nk - 1, kb == 0, vtoks + [("tb", ab)], [("ps", ob)], skip=True)
                if kb == 0 and hh == 1:
                    dve(C("tensor_copy", out=osbT[:, pr, :], in_=banks[ob][:]), [("ps", ob)], [("osb", pr)])

            stages = [s0, s1, s2, s3, s4, s5]
            n = len(units)
            for it in range(n + len(stages) - 1):
                for k, st in enumerate(stages):
                    j = it - k
                    if 0 <= j < n:
                        st(units[j])
            if T == 0:
                dbg("osbT", osbT[:], [128, 8, TT], OSB_TOKS)

            BR = [(7168, b_co, ("w", "co", 0), convT, [("big", 8 + c) for c in range(8)], 3),
                  (8192, b_so, ("w", "so", 0), osbT, OSB_TOKS, 4),
                  (9216, b_xo, ("w", "xo", 0), oxT, [("big", 24 + c) for c in range(8)], 4)]
            for g4 in range(4 if lvl >= 6 else 0):
                slots = []
                for (gcol, wy, wtok, _, _, gseg) in BR:
                    sg = stream(b_in, 0, gcol + g4 * 256, ("w", "in", gseg), ncols=256)
                    sy = stream(wy, 0, g4 * 256, wtok, ncols=256)
                    slots.append((sg, sy))
                for cc in range(2):
                    c = g4 * 2 + cc
                    gts = []
                    for br in range(3):
                        (sg, sgt), _ = slots[br]
                        bk = nb()
                        mm8_feat(bk, sg, sgt, cc, hT, HT_TOKS)
                        gi = ntf()
                        act(C("activation", out=tf[gi][:], in_=banks[bk][:], func=AF.Exp, scale=-1.0),
                            [("ps", bk)], [("tf", gi)])
                        pool(C("tensor_scalar", out=tf[gi][:], in0=tf[gi][:], scalar1=1.0, scalar2=None,
                                                              op0=ALU.add), [("tf", gi)], [("tf", gi)])
                        dve(C("reciprocal", out=tf[gi][:], in_=tf[gi][:]), [("tf", gi)], [("tf", gi)])
                        gts.append(gi)
                    prev = None
                    for br in range(3):
                        _, (sy, syt) = slots[br]
                        _, _, _, inT, in_toks, _ = BR[br]
                        bk = nb()
                        mm8_feat(bk, sy, syt, cc, inT, in_toks)
                        gi = gts[br]
                        dve(C("tensor_tensor", out=tf[gi][:], in0=tf[gi][:], in1=banks[bk][:],
                                                                    op=ALU.mult), [("tf", gi), ("ps", bk)], [("tf", gi)])
                        if br == 1:
                            pool(C("tensor_tensor", out=tf[gts[0]][:], in0=tf[gts[0]][:], in1=tf[gi][:], op=ALU.add),
                                 [("tf", gts[0]), ("tf", gi)], [("tf", gts[0])])
                        if br == 2:
                            pool(C("tensor_tensor", out=mergedT[:, c, :], in0=tf[gts[0]][:], in1=tf[gi][:], op=ALU.add),
                                 [("tf", gts[0]), ("tf", gi)], [("big", 16 + c)])
            MG_TOKS = [("big", 16 + c) for c in range(8)]
            if T == 0:
                dbg("mergedT", mergedT, [128, 8, TT], MG_TOKS)

            for half in range(2 if lvl >= 7 else 0):
                slot, stok = stream(b_o, 0, half * 512, ("w", "o", 0))
                for b in range(4):
                    bk = nb()
                    mm8_tok(bk, mergedT, MG_TOKS, b, slot, stok)
                    dve(C("tensor_tensor", out=xt[:, b, half * 512:(half + 1) * 512],
                                                                         in0=xt[:, b, half * 512:(half + 1) * 512],
                                                                         in1=banks[bk][:], op=ALU.add),
                        [("xt", b), ("ps", bk)], [("xt", b)])
            if T == 0:
                dbg("x1", xt[:], [128, 4, D], [("xt", b) for b in range(4)])

            for b in range(4 if lvl >= 8 else 0):
                norm_transpose(xt[:, b, :], [("xt", b)], gmlp, hT, b * 128, ("hT", b))
            for ublk in range(8 if lvl >= 8 else 0):
                slot, stok = stream(b_up, 0, ublk * 512, ("w", "up", ublk // 4))
                for cc in range(4):
                    j = ublk * 4 + cc
                    bk = nb()
                    mm8_feat(bk, slot, stok, cc, hT, HT_TOKS)
                    ri = ntf()
                    act(C("activation", out=tf[ri][:], in_=banks[bk][:], func=AF.Relu),
                        [("ps", bk)], [("tf", ri)])
                    pool(C("tensor_tensor", out=uT[:, j, :], in0=tf[ri][:], in1=tf[ri][:], op=ALU.mult),
                         [("tf", ri)], [("big", j)])
            for half in range(2 if lvl >= 9 else 0):
                accb = [4 * half + b for b in range(4)]
                for rg in range(4):
                    slot, stok = stream(b_dn, rg * 1024, half * 512, ("w", "dn", rg))
                    for b in range(4):
                        for kc in range(8):
                            j = rg * 8 + kc
                            mm(banks[accb[b]][:], uT[:, j, b * 128:(b + 1) * 128], slot[:, kc, :],
                               rg == 0 and kc == 0, rg == 3 and kc == 7, stok + [("big", j)], [("ps", accb[b])])
                for b in range(4):
                    dve(C("tensor_tensor", out=xt[:, b, half * 512:(half + 1) * 512],
                          in0=xt[:, b, half * 512:(half + 1) * 512], in1=banks[accb[b]][:], op=ALU.add),
                        [("xt", b)] + [("ps", accb[bb]) for bb in range(4)], [("xt", b)])
            for b in range(4):
                o = P.op("pool", C("dma_start", out=out_d[t0 + b * 128:t0 + (b + 1) * 128, :], in_=xt[:, b, :]),
                         reads=[("xt", b)], dma=True)
                final_ops.append(o)

        be = {"pe": block.tensor, "act": block.scalar, "dve": block.vector, "pool": block.gpsimd, "sp": block.sync}
        P.emit(be, sems, dsems, {"pool": final_ops})
    return nc, dbg_outs


def _consts():
    j = np.arange(128)[:, None]
    s = np.arange(128)[None, :]
    c = np.zeros((128, 5, 128), np.float32)
    c[:, 0, :] = np.eye(128)
    c[:, 1, :] = -1.0 * (j >= s)
    c[:, 2, :] = -1.0
    c[:, 3, :] = 1.0
    c[:, 4, :] = NEG * (j >= s)
    return c.astype(ml_dtypes.bfloat16)


def _vecs(g_mix, g_mem, g_mlp, conv_w, q_norm_g, k_norm_g):
    v = np.zeros((128, 52), np.float32)
    v[:, 0:8] = g_mix.reshape(8, 128).T
    v[:, 8:16] = g_mem.reshape(8, 128).T
    v[:, 16:24] = g_mlp.reshape(8, 128).T
    v[:, 24:48] = conv_w.reshape(3, 8, 128).transpose(2, 1, 0).reshape(128, 24)
    v[:, 48:50] = q_norm_g.reshape(2, 128).T
    v[:, 50:52] = k_norm_g.reshape(2, 128).T
    return v


_NC_CACHE = {}


def kernel(x, mem, g_mix, g_mem, w_in, conv_w, w_conv_out, w_sb_out, q_norm_g, k_norm_g,
           w_mem_kv, w_x_out, w_out, g_mlp, w_up, w_down):
    f = lambda a: np.ascontiguousarray(np.asarray(a, dtype=np.float32))
    if "nc" not in _NC_CACHE:
        _NC_CACHE["nc"] = build()[0]
    nc = _NC_CACHE["nc"]
    shared = {
        "w_in": f(w_in[0]), "w_conv_out": f(w_conv_out[0]), "w_sb_out": f(w_sb_out[0]),
        "w_x_out": f(w_x_out[0]), "w_out": f(w_out[0]), "w_mem_kv": f(w_mem_kv[0]),
        "w_up": f(w_up[0]), "w_down": f(w_down[0]),
        "vec": _vecs(f(g_mix[0]), f(g_mem[0]), f(g_mlp[0]), f(conv_w[0]), f(q_norm_g[0]), f(k_norm_g[0])),
        "cst": _consts(),
    }
    x = f(x)
    mem = f(mem)
    in_maps = []
    for b in range(8):
        m = dict(shared)
        m["x"] = x[b]
        m["mem"] = mem[b]
        in_maps.append(m)
    res = run_bass_kernel_spmd(nc, in_maps, core_ids=list(range(8)))
    return np.stack([np.asarray(r["out"], dtype=np.float32) for r in res.results], axis=0)
```

```python
import numpy as np
import ml_dtypes
from contextlib import ExitStack
import concourse.bass as bass
import concourse.mybir as mybir
from concourse.bass_utils import run_bass_kernel_spmd

F32 = mybir.dt.float32
BF = mybir.dt.bfloat16
AF = mybir.ActivationFunctionType
ALU = mybir.AluOpType

ENGS = ("pe", "act", "dve", "pool", "sp")

S = 2048
D = 1024
TT = 512
NT = S // TT
MEM = 256
DFF = 4096
INC = 10240
EPS = 1e-6
NEG = -30000.0


class Op:
    __slots__ = ("eng", "fn", "deps", "signal", "sval", "is_dma", "dsem", "dval")

    def __init__(self, eng, fn, is_dma):
        self.eng = eng
        self.fn = fn
        self.deps = []
        self.signal = False
        self.sval = 0
        self.is_dma = is_dma
        self.dsem = None
        self.dval = 0


class Prog:
    def __init__(self, n_dma_sems=None, self_sync=True):
        self.ops = {e: [] for e in ENGS}
        self.last_w = {}
        self.readers = {}
        self.n_dma_sems = n_dma_sems or {"sp": 8, "pool": 24, "act": 4}
        self.dma_count = {}
        self.dma_rr = {e: 0 for e in ENGS}
        self.dma_last = {}
        self.self_sync = self_sync

    def op(self, eng, fn, reads=(), writes=(), dma=False):
        o = Op(eng, fn, dma)
        deps = []
        for t in reads:
            w = self.last_w.get(t)
            if w is not None:
                deps.append(w)
        for t in writes:
            w = self.last_w.get(t)
            if w is not None:
                deps.append(w)
            deps.extend(self.readers.get(t, ()))
        if dma:
            k = self.dma_rr[eng] % self.n_dma_sems[eng]
            self.dma_rr[eng] += 1
            key = (eng, k)
            self.dma_count[key] = self.dma_count.get(key, 0) + 1
            o.dsem = key
            o.dval = 16 * self.dma_count[key]
            prev = self.dma_last.get(key)
            if prev is not None:
                deps.append(prev)
            self.dma_last[key] = o
        seen = set()
        for d in deps:
            if id(d) in seen:
                continue
            seen.add(id(d))
            if (not d.is_dma) and d.eng == eng:
                if eng == "pe" or not self.self_sync:
                    continue
            o.deps.append(d)
            if not d.is_dma:
                d.signal = True
        for t in reads:
            self.readers.setdefault(t, []).append(o)
        for t in writes:
            self.last_w[t] = o
            self.readers[t] = []
        self.ops[eng].append(o)
        return o

    def emit(self, block_engines, sems, dma_sems, final_waits):
        for e in ENGS:
            c = 0
            for o in self.ops[e]:
                if o.signal and not o.is_dma:
                    c += 1
                    o.sval = c

        def dep_key(d):
            if d.is_dma:
                return ("d",) + d.dsem, d.dval
            return ("e", d.eng), d.sval

        def sem_of(k):
            return dma_sems[k[1:]] if k[0] == "d" else sems[k[1]]

        def run(e, eng):
            known = {}
            for o in self.ops[e]:
                need = {}
                for d in o.deps:
                    k, v = dep_key(d)
                    if v > need.get(k, 0):
                        need[k] = v
                for k, v in need.items():
                    if known.get(k, 0) >= v:
                        continue
                    known[k] = v
                    eng.wait_ge(sem_of(k), v)
                ins = o.fn(eng)
                if o.is_dma:
                    ins.then_inc(dma_sems[o.dsem], 16)
                elif o.signal:
                    ins.then_inc(sems[e], 1)
            for d in final_waits.get(e, ()):
                k, v = dep_key(d)
                eng.wait_ge(sem_of(k), v)

        for e in ENGS:
            block_engines[e](lambda eng, e=e: run(e, eng))


def build(ntiles=NT, debug=(), lvl=9, skip=()):
    nc = bass.Bass("TRN2", target_bir_lowering=False)

    def din(name, shape, dt=F32):
        return nc.dram_tensor(name, list(shape), dt, kind="ExternalInput").ap()

    x_d = din("x", [S, D])
    mem_d = din("mem", [MEM, D])
    w_in_d = din("w_in", [D, INC])
    w_co_d = din("w_conv_out", [D, D])
    w_so_d = din("w_sb_out", [D, D])
    w_xo_d = din("w_x_out", [D, D])
    w_o_d = din("w_out", [D, D])
    w_kv_d = din("w_mem_kv", [D, 2 * D])
    w_up_d = din("w_up", [D, DFF])
    w_dn_d = din("w_down", [DFF, D])
    vec_d = din("vec", [128, 52])
    cst_d = din("cst", [128, 5, 128], BF)
    out_d = nc.dram_tensor("out", [S, D], F32, kind="ExternalOutput").ap()

    def dint(name, shape):
        return nc.dram_tensor(name, list(shape), BF, kind="Internal").ap()

    b_in = dint("b_in", [D, INC])
    b_co = dint("b_co", [D, D])
    b_so = dint("b_so", [D, D])
    b_xo = dint("b_xo", [D, D])
    b_o = dint("b_o", [D, D])
    b_kv = dint("b_kv", [D, 2 * D])
    b_up = dint("b_up", [D, DFF])
    b_dn = dint("b_dn", [DFF, D])

    dbg_outs = {}
    P = Prog()
    with ExitStack() as es:
        def sb(name, shape, dt):
            return es.enter_context(nc.sbuf_tensor(name, list(shape), dt))

        xt = sb("xt", [128, 4, D], F32)
        hT = sb("hT", [128, 8, TT], BF)
        big = sb("big", [128, 32, TT], BF)
        osbT = sb("osbT", [128, 8, TT], BF)
        kT = sb("kT", [128, 8, S], BF)
        vA = sb("vA", [128, 16, D], BF)
        knT = sb("knT", [128, 8, MEM], BF)
        vmem = sb("vmem", [128, 2, D], BF)
        NRING = 4
        wring = [sb(f"ws{i}", [128, 8, 512], BF) for i in range(NRING)]
        junk = sb("junk", [128, D], BF)
        hb = [sb(f"hb{i}", [128, D], BF) for i in range(2)]
        NTF = 6
        tf = [sb(f"tf{i}", [128, 512], F32) for i in range(NTF)]
        NTB = 10
        tb = [sb(f"tb{i}", [128, 512], BF) for i in range(NTB)]
        ubuf = [sb(f"ub{i}", [128, 516], F32) for i in range(2)]
        ucar = sb("ucar", [128, 8, 2], F32)
        cst = sb("cst_sb", [128, 5, 128], BF)
        vec = sb("vecs", [128, 52], F32)
        gqk = sb("gqk", [128, 2], F32)
        NSTAT = 8
        stat = sb("stat", [128, NSTAT, 4], F32)
        banks = [es.enter_context(nc.psum_tensor(f"ps{i}", [128, 512], F32)) for i in range(8)]
        sems = {e: es.enter_context(nc.semaphore(f"s_{e}")) for e in ENGS}
        dsems = {}
        for e in ("sp", "pool", "act"):
            for k in range(P.n_dma_sems[e]):
                dsems[(e, k)] = es.enter_context(nc.semaphore(f"d_{e}{k}"))
        block = es.enter_context(nc.Block())

        ident = cst[:, 0, :]
        negU = cst[:, 1, :]
        negones = cst[:, 2, :]
        ones = cst[:, 3, :]
        negmask = cst[:, 4, :]
        gmix = vec[:, 0:8]
        gmem = vec[:, 8:16]
        gmlp = vec[:, 16:24]
        convw = vec[:, 24:48].rearrange("p (c k) -> p c k", k=3)
        qg = vec[:, 48:50]
        kg = vec[:, 50:52]

        qT = big[:, 0:8, :]
        convT = big[:, 8:16, :]
        xqnT = big[:, 16:24, :]
        mergedT = big[:, 16:24, :]
        oxT = big[:, 24:32, :]
        uT = big

        final_ops = []

        state = {"bank": 0, "ring": 0, "stat": 0, "hb": 0, "tf": 0, "tb": 0, "ub": 0, "ev": 0}

        def nb():
            i = state["bank"] % 8
            state["bank"] += 1
            return i

        def ntf():
            i = state["tf"] % NTF
            state["tf"] += 1
            return i

        def ntb():
            i = state["tb"] % NTB
            state["tb"] += 1
            return i

        def dbg(name, ap, shape, reads):
            if name not in debug:
                return
            dt = ap.dtype
            t = nc.dram_tensor("dbg_" + name, list(shape), dt, kind="ExternalOutput").ap()
            dbg_outs[name] = t
            o = P.op("pool", C("dma_start", out=t, in_=ap), reads=reads, dma=True)
            final_ops.append(o)

        def C(meth, *a, **kw):
            return lambda e: getattr(e, meth)(*a, **kw)

        def act(fn, reads, writes):
            return P.op("act", fn, reads, writes)

        def dve(fn, reads, writes):
            return P.op("dve", fn, reads, writes)

        def pool(fn, reads, writes):
            return P.op("pool", fn, reads, writes)

        def mm(out, lhsT, rhs, start, stop, reads, writes, skip=False):
            return P.op("pe", C("matmul", out, lhsT=lhsT, rhs=rhs, start=start, stop=stop, skip_group_check=skip),
                        reads, writes)

        def evac(dst, src, reads, writes, scale=None):
            state["ev"] += 1
            if state["ev"] % 2 == 0:
                if scale is None:
                    act(C("activation", out=dst, in_=src, func=AF.Copy), reads, writes)
                else:
                    act(C("activation", out=dst, in_=src, func=AF.Copy, scale=scale), reads, writes)
            else:
                if scale is None:
                    dve(C("tensor_copy", out=dst, in_=src), reads, writes)
                else:
                    dve(C("tensor_scalar", out=dst, in0=src, scalar1=scale, scalar2=None, op0=ALU.mult),
                        reads, writes)

        def stream(wap, r0, c0, segtok, ncols=512):
            if ncols == 512:
                if state["ring"] % 2:
                    state["ring"] += 1
                h = state["ring"] % (2 * NRING)
                state["ring"] += 2
                dst = wring[h // 2][:, :, :]
                toks = [("ws", h), ("ws", h + 1)]
            else:
                h = state["ring"] % (2 * NRING)
                state["ring"] += 1
                dst = wring[h // 2][:, :, (h % 2) * 256:(h % 2) * 256 + 256]
                toks = [("ws", h)]
            src = wap[r0:r0 + 1024, c0:c0 + ncols].rearrange("(kc p) n -> p kc n", p=128)
            P.op("sp", C("dma_start", out=dst, in_=src), reads=[segtok], writes=toks, dma=True)
            return dst, toks

        def mm8_feat(bank, slot, slot_tok, cc, rhsT, rhs_toks, n=TT):
            for kc in range(8):
                mm(banks[bank][:, 0:n], slot[:, kc, cc * 128:(cc + 1) * 128], rhsT[:, kc, 0:n],
                   kc == 0, kc == 7, slot_tok + rhs_toks, [("ps", bank)])

        def mm8_tok(bank, lhsT_buf, lhs_toks, b, slot, slot_tok, n=512):
            for kc in range(8):
                mm(banks[bank][:, 0:n], lhsT_buf[:, kc, b * 128:(b + 1) * 128], slot[:, kc, 0:n],
                   kc == 0, kc == 7, slot_tok + lhs_toks, [("ps", bank)])

        HT_TOKS = [("hT", b) for b in range(4)]

        def rstd_small(src_ap, n_inv, reads):
            si = state["stat"] % NSTAT
            state["stat"] += 1
            ss = stat[:, si, 0:1]
            lv = stat[:, si, 1:2]
            rs = stat[:, si, 2:3]
            tk = ("stat", si)
            act(C("activation", out=junk[:], in_=src_ap, func=AF.Square, accum_out=ss),
                reads, ["junk", tk])
            act(C("activation", out=lv, in_=ss, func=AF.Ln, scale=n_inv, bias=EPS), [tk], [tk])
            act(C("activation", out=rs, in_=lv, func=AF.Exp, scale=-0.5), [tk], [tk])
            return rs, tk

        def norm_transpose(src_ap, src_toks, gv, dstT, col0, dst_tok):
            rs, tk = rstd_small(src_ap, 1.0 / D, src_toks)
            hi = state["hb"] % 2
            state["hb"] += 1
            h = hb[hi]
            act(C("activation", out=h[:], in_=src_ap, func=AF.Copy, scale=rs), src_toks + [tk], [("hb", hi)])
            bk = nb()
            pb = banks[bk][:].bitcast(BF)
            for c in range(8):
                P.op("pe", C("transpose", out=pb[:, c * 128:(c + 1) * 128],
                                                      in_=h[:, c * 128:(c + 1) * 128], identity=ident),
                     [("hb", hi), "cst"], [("ps", bk)])
            dve(C("tensor_tensor", out=dstT[:, :, col0:col0 + 128],
                                          in0=pb.rearrange("p (c t) -> p c t", c=8),
                                          in1=gv.unsqueeze(2).to_broadcast([128, 8, 128]), op=ALU.mult),
                [("ps", bk), "vec"], [dst_tok])

        def rstd_bcast(bankC, n_inv, n):
            ti = ntf()
            t = tf[ti]
            act(C("activation", out=t[:, 0:n], in_=banks[bankC][:, 0:n], func=AF.Ln, scale=n_inv, bias=EPS),
                [("ps", bankC)], [("tf", ti)])
            act(C("activation", out=t[:, 0:n], in_=t[:, 0:n], func=AF.Exp, scale=-0.5),
                [("tf", ti)], [("tf", ti)])
            return t, ("tf", ti)

        P.op("sp", C("dma_start", out=cst[:], in_=cst_d), writes=["cst"], dma=True)
        P.op("sp", C("dma_start", out=vec[:], in_=vec_d), writes=["vec"], dma=True)
        for mb in range(2):
            P.op("sp", C("dma_start", out=xt[:, mb, :], in_=mem_d[mb * 128:(mb + 1) * 128, :]),
                 writes=[("xt", mb)], dma=True)

        def cast(dst, src, tok):
            P.op("pool", C("dma_start", out=dst, in_=src), writes=[tok], dma=True)

        cast(b_kv[:, :], w_kv_d[:, :], ("w", "kv", 0))
        for sgm in (1, 2, 0, 3, 4):
            cast(b_in[:, sgm * 2048:(sgm + 1) * 2048], w_in_d[:, sgm * 2048:(sgm + 1) * 2048], ("w", "in", sgm))
        cast(b_co[:, :], w_co_d[:, :], ("w", "co", 0))
        cast(b_so[:, :], w_so_d[:, :], ("w", "so", 0))
        cast(b_xo[:, :], w_xo_d[:, :], ("w", "xo", 0))
        cast(b_o[:, :], w_o_d[:, :], ("w", "o", 0))
        for sgm in range(2):
            cast(b_up[:, sgm * 2048:(sgm + 1) * 2048], w_up_d[:, sgm * 2048:(sgm + 1) * 2048], ("w", "up", sgm))
        for rg in range(4):
            cast(b_dn[rg * 1024:(rg + 1) * 1024, :], w_dn_d[rg * 1024:(rg + 1) * 1024, :], ("w", "dn", rg))

        pool(C("memset", ucar[:], 0.0), [], ["ucar"])
        dve(C("scalar_tensor_tensor", out=gqk[:], in0=qg, scalar=1.0 / 16.0, in1=kg,
                                             op0=ALU.mult, op1=ALU.mult), ["vec"], ["gqk"])

        memT = osbT[:, :, 0:MEM]
        OSB_TOKS = [("osb", c) for c in range(8)]
        for mb in range(2 if lvl >= 1 else 0):
            norm_transpose(xt[:, mb, :], [("xt", mb)], gmem, memT, mb * 128, ("memT", mb))
        MEMT_TOKS = [("memT", 0), ("memT", 1)] + OSB_TOKS
        for blk in range(2 if lvl >= 1 else 0):
            slot, stok = stream(b_kv, 0, blk * 512, ("w", "kv", 0))
            for hl in range(2):
                hx = blk * 2 + hl
                bk = [nb(), nb()]
                sq = []
                for dc in range(2):
                    mm8_feat(bk[dc], slot, stok, hl * 2 + dc, memT, MEMT_TOKS, n=MEM)
                    ti = ntb()
                    act(C("activation", out=tb[ti][:, 0:MEM], in_=banks[bk[dc]][:, 0:MEM],
                                                             func=AF.Square), [("ps", bk[dc])], [("tb", ti)])
                    sq.append(ti)
                bC = nb()
                for dc in range(2):
                    mm(banks[bC][:, 0:MEM], ones, tb[sq[dc]][:, 0:MEM], dc == 0, dc == 1,
                       ["cst", ("tb", sq[dc])], [("ps", bC)])
                rq, rtok = rstd_bcast(bC, 1.0 / 256, MEM)
                for dc in range(2):
                    dve(C("scalar_tensor_tensor", out=knT[:, 2 * hx + dc, :], in0=banks[bk[dc]][:, 0:MEM],
                                                                scalar=gqk[:, dc:dc + 1], in1=rq[:, 0:MEM],
                                                                op0=ALU.mult, op1=ALU.mult),
                        [("ps", bk[dc]), rtok, "gqk"], [("knT", 2 * hx + dc)])
        for half in range(2 if lvl >= 1 else 0):
            slot, stok = stream(b_kv, 0, 1024 + half * 512, ("w", "kv", 0))
            for mc in range(2):
                bk = nb()
                mm8_tok(bk, memT, MEMT_TOKS, mc, slot, stok)
                evac(vmem[:, mc, half * 512:(half + 1) * 512], banks[bk][:], [("ps", bk)], [("vmem", mc, half)])
        KN_TOKS = [("knT", c) for c in range(8)]
        VM_TOKS = [("vmem", mc, h) for mc in range(2) for h in range(2)]
        dbg("knT", knT[:], [128, 8, MEM], KN_TOKS)
        dbg("vmem", vmem[:], [128, 2, D], VM_TOKS)

        for T in range(ntiles):
            t0 = T * TT
            for b in range(4):
                P.op("act", C("dma_start", out=xt[:, b, :], in_=x_d[t0 + b * 128:t0 + (b + 1) * 128, :]),
                     writes=[("xt", b)], dma=True)
            for b in range(4 if lvl >= 2 else 0):
                norm_transpose(xt[:, b, :], [("xt", b)], gmix, hT, b * 128, ("hT", b))
            if T == 0:
                dbg("hT", hT[:], [128, 8, TT], HT_TOKS)

            for blk in range(2 if lvl >= 2 else 0):
                slot, stok = stream(b_in, 0, 3072 + blk * 512, ("w", "in", 1))
                for cc in range(4):
                    c = blk * 4 + cc
                    bk = nb()
                    mm8_feat(bk, slot, stok, cc, hT, HT_TOKS)
                    evac(qT[:, c, :], banks[bk][:], [("ps", bk)], [("big", c)], scale=0.125)
            for blk in range(2 if lvl >= 2 else 0):
                slot, stok = stream(b_in, 0, 4096 + blk * 512, ("w", "in", 2))
                for cc in range(4):
                    c = blk * 4 + cc
                    bk = nb()
                    mm8_feat(bk, slot, stok, cc, hT, HT_TOKS)
                    evac(kT[:, c, t0:t0 + TT], banks[bk][:], [("ps", bk)], [("kT", c, T)])
            for half in range(2 if lvl >= 2 else 0):
                slot, stok = stream(b_in, 0, 5120 + half * 512, ("w", "in", 2))
                for b in range(4):
                    bk = nb()
                    mm8_tok(bk, hT, HT_TOKS, b, slot, stok)
                    evac(vA[:, T * 4 + b, half * 512:(half + 1) * 512], banks[bk][:], [("ps", bk)],
                         [("vA", T * 4 + b, half)])
            if T == 0:
                dbg("qT", qT, [128, 8, TT], [("big", c) for c in range(8)])
                dbg("kT0", kT[:, :, 0:TT], [128, 8, TT], [("kT", c, 0) for c in range(8)])
                dbg("v0", vA[:, 0:4, :], [128, 4, D], [("vA", b, h) for b in range(4) for h in range(2)])

            for j in range(2 if lvl >= 3 else 0):
                sl_h, tk_h = stream(b_in, 0, j * 512, ("w", "in", 0))
                sl_c, tk_c = stream(b_in, 0, 2048 + j * 512, ("w", "in", 1))
                sl_b, tk_b = stream(b_in, 0, 1024 + j * 512, ("w", "in", 0))
                for cc in range(4):
                    c = j * 4 + cc
                    bh = nb()
                    mm8_feat(bh, sl_h, tk_h, cc, hT, HT_TOKS)
                    thi = ntf()
                    act(C("activation", out=tf[thi][:], in_=banks[bh][:], func=AF.Copy),
                        [("ps", bh)], [("tf", thi)])
                    bc = nb()
                    mm8_feat(bc, sl_c, tk_c, cc, hT, HT_TOKS)
                    ui = state["ub"] % 2
                    state["ub"] += 1
                    ub = ubuf[ui]
                    utok = ("ub", ui)
                    pool(C("tensor_copy", out=ub[:, 0:2], in_=ucar[:, c, :]), ["ucar"], [utok])
                    dve(C("tensor_tensor", out=ub[:, 2:514], in0=tf[thi][:],
                                                                         in1=banks[bc][:], op=ALU.mult),
                        [("tf", thi), ("ps", bc)], [utok])
                    pool(C("tensor_copy", out=ucar[:, c, :], in_=ub[:, 512:514]), [utok], ["ucar"])
                    ai = ntf()
                    acc = tf[ai]
                    atok = ("tf", ai)
                    dve(C("tensor_scalar", out=acc[:], in0=ub[:, 2:514],
                                                                       scalar1=convw[:, c, 2:3], scalar2=None,
                                                                       op0=ALU.mult), [utok, "vec"], [atok])
                    dve(C("scalar_tensor_tensor", out=acc[:], in0=ub[:, 1:513],
                                                                              scalar=convw[:, c, 1:2], in1=acc[:],
                                                                              op0=ALU.mult, op1=ALU.add),
                        [utok, "vec", atok], [atok])
                    dve(C("scalar_tensor_tensor", out=acc[:], in0=ub[:, 0:512],
                                                                              scalar=convw[:, c, 0:1], in1=acc[:],
                                                                              op0=ALU.mult, op1=ALU.add),
                        [utok, "vec", atok], [atok])
                    bb = nb()
                    mm8_feat(bb, sl_b, tk_b, cc, hT, HT_TOKS)
                    dve(C("tensor_tensor", out=convT[:, c, :], in0=acc[:], in1=banks[bb][:],
                                                                       op=ALU.mult),
                        [atok, ("ps", bb)], [("big", 8 + c)])
            if T == 0:
                dbg("convT", convT, [128, 8, TT], [("big", 8 + c) for c in range(8)])

            for blk in range(2 if lvl >= 4 else 0):
                slot, stok = stream(b_in, 0, 6144 + blk * 512, ("w", "in", 3))
                for hl in range(2):
                    hx = blk * 2 + hl
                    bk = [nb(), nb()]
                    sq = []
                    for dc in range(2):
                        mm8_feat(bk[dc], slot, stok, hl * 2 + dc, hT, HT_TOKS)
                        ti = ntb()
                        act(C("activation", out=tb[ti][:], in_=banks[bk[dc]][:],
                                                                        func=AF.Square), [("ps", bk[dc])], [("tb", ti)])
                        sq.append(ti)
                    bC = nb()
                    for dc in range(2):
                        mm(banks[bC][:], ones, tb[sq[dc]][:], dc == 0, dc == 1, ["cst", ("tb", sq[dc])], [("ps", bC)])
                    rq, rtok = rstd_bcast(bC, 1.0 / 256, TT)
                    for dc in range(2):
                        dve(C("tensor_tensor", out=xqnT[:, 2 * hx + dc, :],
                                                                                  in0=rq[:], in1=banks[bk[dc]][:],
                                                                                  op=ALU.mult),
                            [("ps", bk[dc]), rtok], [("big", 16 + 2 * hx + dc)])
            if T == 0:
                dbg("xqnT", xqnT, [128, 8, TT], [("big", 16 + c) for c in range(8)])
            for hx in range(4 if lvl >= 4 else 0):
                pts = []
                for mc in range(2):
                    bS = nb()
                    for dc in range(2):
                        mm(banks[bS][:], knT[:, 2 * hx + dc, mc * 128:(mc + 1) * 128], xqnT[:, 2 * hx + dc, :],
                           dc == 0, dc == 1, KN_TOKS + [("big", 16 + 2 * hx + dc)], [("ps", bS)])
                    ti = ntb()
                    act(C("activation", out=tb[ti][:], in_=banks[bS][:], func=AF.Exp),
                        [("ps", bS)], [("tb", ti)])
                    pts.append(ti)
                bD = nb()
                for mc in range(2):
                    mm(banks[bD][:], ones, tb[pts[mc]][:], mc == 0, mc == 1, ["cst", ("tb", pts[mc])], [("ps", bD)])
                ri = ntf()
                dve(C("reciprocal", out=tf[ri][:], in_=banks[bD][:]), [("ps", bD)], [("tf", ri)])
                for dc in range(2):
                    bO = nb()
                    for mc in range(2):
                        mm(banks[bO][:], vmem[:, mc, hx * 256 + dc * 128: hx * 256 + (dc + 1) * 128], tb[pts[mc]][:],
                           mc == 0, mc == 1, VM_TOKS + [("tb", pts[mc])], [("ps", bO)])
                    dve(C("tensor_tensor", out=oxT[:, 2 * hx + dc, :], in0=tf[ri][:],
                                                                              in1=banks[bO][:], op=ALU.mult),
                        [("tf", ri), ("ps", bO)], [("big", 24 + 2 * hx + dc)])
            if T == 0:
                dbg("oxT", oxT, [128, 8, TT], [("big", 24 + c) for c in range(8)])

            nk = 4 * T + 4
            units = []
            for pr in range(8 if (lvl >= 5 and "sb" not in skip) else 0):
                for kb in range(nk - 1, -1, -1):
                    for hh in range(2):
                        units.append({"pr": pr, "hh": hh, "kb": kb, "i": len(units)})
            ZB = [0, 1, 2, 3, 4, 5]
            OB = [6, 7]
            EB = [0, 1, 2, 3]
            SPB = [0, 1, 2, 3]
            AB = [4, 5, 6, 7]
            RB = [8, 9]

            def kv_toks(u):
                pr, kb = u["pr"], u["kb"]
                return ("kT", pr, kb // 4), [("vA", kb, (2 * pr) // 8)]

            def s0(u):
                pr, hh, kb, i = u["pr"], u["hh"], u["kb"], u["i"]
                c0 = max(0, kb - 4 * T) * 128
                u["c0"] = c0
                zb = ZB[i % len(ZB)]
                u["zb"] = zb
                diag = kb >= 4 * T
                p0 = 64 * hh
                ktok, _ = kv_toks(u)
                mm(banks[zb][:, c0:512], kT[p0:p0 + 64, pr, kb * 128:(kb + 1) * 128], qT[p0:p0 + 64, pr, c0:512],
                   True, True, [ktok, ("big", pr)], [("ps", zb)])
                if diag:
                    mm(banks[zb][:, c0:c0 + 128], ident, negmask, False, True, ["cst"], [("ps", zb)], skip=True)

            def s1(u):
                i, c0, zb = u["i"], u["c0"], u["zb"]
                eb = EB[i % len(EB)]
                u["eb"] = eb
                act(C("activation", out=tf[eb][:, c0:512], in_=banks[zb][:, c0:512], func=AF.Exp),
                    [("ps", zb)], [("tf", eb)])

            def s2(u):
                i, c0, eb = u["i"], u["c0"], u["eb"]
                sp = SPB[i % len(SPB)]
                u["sp"] = sp
                act(C("activation", out=tb[sp][:, c0:512], in_=tf[eb][:, c0:512], func=AF.Ln, bias=1.0, scale=1.0),
                    [("tf", eb)], [("tb", sp)])

            def s3(u):
                hh, kb, c0, zb, sp = u["hh"], u["kb"], u["c0"], u["zb"], u["sp"]
                rb = RB[hh]
                has_r = kb < nk - 1
                mm(banks[zb][:, c0:512], negU, tb[sp][:, c0:512], False, not has_r, ["cst", ("tb", sp)], [("ps", zb)],
                   skip=True)
                if has_r:
                    mm(banks[zb][:, c0:512], negones, tb[rb][:, c0:512], False, True, ["cst", ("tb", rb)], [("ps", zb)],
                       skip=True)
                if kb == nk - 1:
                    pool(C("memset", tb[rb][:], 0.0), [], [("tb", rb)])
                if kb > 0:
                    dve(C("tensor_tensor", out=tb[rb][:, c0:512], in0=tb[rb][:, c0:512], in1=tb[sp][:, c0:512],
                                                   op=ALU.add), [("tb", rb), ("tb", sp)], [("tb", rb)])

            def s4(u):
                i, c0, zb = u["i"], u["c0"], u["zb"]
                ab = AB[i % len(AB)]
                u["ab"] = ab
                act(C("activation", out=tb[ab][:, c0:512], in_=banks[zb][:, c0:512], func=AF.Exp),
                    [("ps", zb)], [("tb", ab)])

            def s5(u):
                pr, hh, kb, c0, ab = u["pr"], u["hh"], u["kb"], u["c0"], u["ab"]
                ob = OB[pr % 2]
                hd = 2 * pr + hh
                _, vtoks = kv_toks(u)
                mm(banks[ob][64 * hh:64 * hh + 64, c0:512], vA[:, kb, hd * 64:(hd + 1) * 64], tb[ab][:, c0:512],
                   kb == nk - 1, kb == 0, vtoks + [("tb", ab)], [("ps", ob)], skip=True)
                if kb == 0 and hh == 1:
                    dve(C("tensor_copy", out=osbT[:, pr, :], in_=banks[ob][:]), [("ps", ob)], [("osb", pr)])

            npairs = len(units) // 2
            for it in range(npairs + 2):
                for grp, lag in (((s0, s1, s2), 0), ((s3, s4), 1), ((s5,), 2)):
                    pi = it - lag
                    if 0 <= pi < npairs:
                        for st in grp:
                            for u in (units[2 * pi], units[2 * pi + 1]):
                                st(u)
            if T == 0:
                dbg("osbT", osbT[:], [128, 8, TT], OSB_TOKS)

            BR = [(7168, b_co, ("w", "co", 0), convT, [("big", 8 + c) for c in range(8)], 3),
                  (8192, b_so, ("w", "so", 0), osbT, OSB_TOKS, 4),
                  (9216, b_xo, ("w", "xo", 0), oxT, [("big", 24 + c) for c in range(8)], 4)]
            for g4 in range(4 if lvl >= 6 else 0):
                slots = []
                for (gcol, wy, wtok, _, _, gseg) in BR:
                    sg = stream(b_in, 0, gcol + g4 * 256, ("w", "in", gseg), ncols=256)
                    sy = stream(wy, 0, g4 * 256, wtok, ncols=256)
                    slots.append((sg, sy))
                for cc in range(2):
                    c = g4 * 2 + cc
                    gts = []
                    for br in range(3):
                        (sg, sgt), _ = slots[br]
                        bk = nb()
                        mm8_feat(bk, sg, sgt, cc, hT, HT_TOKS)
                        gi = ntf()
                        act(C("activation", out=tf[gi][:], in_=banks[bk][:], func=AF.Tanh, scale=0.5),
                            [("ps", bk)], [("tf", gi)])
                        gts.append(gi)
                    prev = None
                    for br in range(3):
                        _, (sy, syt) = slots[br]
                        _, _, _, inT, in_toks, _ = BR[br]
                        bk = nb()
                        mm8_feat(bk, sy, syt, cc, inT, in_toks)
                        gi = gts[br]
                        dve(C("scalar_tensor_tensor", out=tf[gi][:], in0=tf[gi][:], scalar=1.0, in1=banks[bk][:],
                              op0=ALU.add, op1=ALU.mult), [("tf", gi), ("ps", bk)], [("tf", gi)])
                        if br == 1:
                            pool(C("tensor_tensor", out=tf[gts[0]][:], in0=tf[gts[0]][:], in1=tf[gi][:], op=ALU.add),
                                 [("tf", gts[0]), ("tf", gi)], [("tf", gts[0])])
                        if br == 2:
                            pool(C("tensor_tensor", out=mergedT[:, c, :], in0=tf[gts[0]][:], in1=tf[gi][:], op=ALU.add),
                                 [("tf", gts[0]), ("tf", gi)], [("big", 16 + c)])
            MG_TOKS = [("big", 16 + c) for c in range(8)]
            if T == 0:
                dbg("mergedT", mergedT, [128, 8, TT], MG_TOKS)

            for half in range(2 if lvl >= 7 else 0):
                slot, stok = stream(b_o, 0, half * 512, ("w", "o", 0))
                for b in range(4):
                    bk = nb()
                    mm8_tok(bk, mergedT, MG_TOKS, b, slot, stok)
                    dve(C("scalar_tensor_tensor", out=xt[:, b, half * 512:(half + 1) * 512], in0=banks[bk][:], scalar=0.5,
                          in1=xt[:, b, half * 512:(half + 1) * 512], op0=ALU.mult, op1=ALU.add),
                        [("xt", b), ("ps", bk)], [("xt", b)])
            if T == 0:
                dbg("x1", xt[:], [128, 4, D], [("xt", b) for b in range(4)])

            for b in range(4 if lvl >= 8 else 0):
                norm_transpose(xt[:, b, :], [("xt", b)], gmlp, hT, b * 128, ("hT", b))
            for ublk in range(8 if lvl >= 8 else 0):
                slot, stok = stream(b_up, 0, ublk * 512, ("w", "up", ublk // 4))
                for cc in range(4):
                    j = ublk * 4 + cc
                    bk = nb()
                    mm8_feat(bk, slot, stok, cc, hT, HT_TOKS)
                    ri = ntf()
                    act(C("activation", out=tf[ri][:], in_=banks[bk][:], func=AF.Relu),
                        [("ps", bk)], [("tf", ri)])
                    dve(C("tensor_tensor", out=uT[:, j, :], in0=tf[ri][:], in1=banks[bk][:], op=ALU.mult),
                        [("tf", ri), ("ps", bk)], [("big", j)])
            for half in range(2 if lvl >= 9 else 0):
                accb = [4 * half + b for b in range(4)]
                for rg in range(4):
                    slot, stok = stream(b_dn, rg * 1024, half * 512, ("w", "dn", rg))
                    for b in range(4):
                        for kc in range(8):
                            j = rg * 8 + kc
                            mm(banks[accb[b]][:], uT[:, j, b * 128:(b + 1) * 128], slot[:, kc, :],
                               rg == 0 and kc == 0, rg == 3 and kc == 7, stok + [("big", j)], [("ps", accb[b])])
                for b in range(4):
                    dve(C("tensor_tensor", out=xt[:, b, half * 512:(half + 1) * 512],
                          in0=xt[:, b, half * 512:(half + 1) * 512], in1=banks[accb[b]][:], op=ALU.add),
                        [("xt", b)] + [("ps", accb[bb]) for bb in range(4)], [("xt", b)])
            for b in range(4):
                o = P.op("pool", C("dma_start", out=out_d[t0 + b * 128:t0 + (b + 1) * 128, :], in_=xt[:, b, :]),
                         reads=[("xt", b)], dma=True)
                final_ops.append(o)

        be = {"pe": block.tensor, "act": block.scalar, "dve": block.vector, "pool": block.gpsimd, "sp": block.sync}
        P.emit(be, sems, dsems, {"pool": final_ops})
    return nc, dbg_outs


def _consts():
    j = np.arange(128)[:, None]
    s = np.arange(128)[None, :]
    c = np.zeros((128, 5, 128), np.float32)
    c[:, 0, :] = np.eye(128)
    c[:, 1, :] = -1.0 * (j >= s)
    c[:, 2, :] = -1.0
    c[:, 3, :] = 1.0
    c[:, 4, :] = NEG * (j >= s)
    return c.astype(ml_dtypes.bfloat16)


def _vecs(g_mix, g_mem, g_mlp, conv_w, q_norm_g, k_norm_g):
    v = np.zeros((128, 52), np.float32)
    v[:, 0:8] = g_mix.reshape(8, 128).T
    v[:, 8:16] = g_mem.reshape(8, 128).T
    v[:, 16:24] = g_mlp.reshape(8, 128).T
    v[:, 24:48] = conv_w.reshape(3, 8, 128).transpose(2, 1, 0).reshape(128, 24)
    v[:, 48:50] = q_norm_g.reshape(2, 128).T
    v[:, 50:52] = k_norm_g.reshape(2, 128).T
    return v


_NC_CACHE = {}


def kernel(x, mem, g_mix, g_mem, w_in, conv_w, w_conv_out, w_sb_out, q_norm_g, k_norm_g,
           w_mem_kv, w_x_out, w_out, g_mlp, w_up, w_down):
    f = lambda a: np.ascontiguousarray(np.asarray(a, dtype=np.float32))
    if "nc" not in _NC_CACHE:
        _NC_CACHE["nc"] = build()[0]
    nc = _NC_CACHE["nc"]
    shared = {
        "w_in": f(w_in[0]), "w_conv_out": f(w_conv_out[0]), "w_sb_out": f(w_sb_out[0]),
        "w_x_out": f(w_x_out[0]), "w_out": f(w_out[0]), "w_mem_kv": f(w_mem_kv[0]),
        "w_up": f(w_up[0]), "w_down": f(w_down[0]),
        "vec": _vecs(f(g_mix[0]), f(g_mem[0]), f(g_mlp[0]), f(conv_w[0]), f(q_norm_g[0]), f(k_norm_g[0])),
        "cst": _consts(),
    }
    x = f(x)
    mem = f(mem)
    in_maps = []
    for b in range(8):
        m = dict(shared)
        m["x"] = x[b]
        m["mem"] = mem[b]
        in_maps.append(m)
    res = run_bass_kernel_spmd(nc, in_maps, core_ids=list(range(8)))
    return np.stack([np.asarray(r["out"], dtype=np.float32) for r in res.results], axis=0)
```

```python
import numpy as np
import ml_dtypes
from contextlib import ExitStack
import concourse.bass as bass
import concourse.mybir as mybir
from concourse.bass_utils import run_bass_kernel_spmd

F32 = mybir.dt.float32
BF = mybir.dt.bfloat16
AF = mybir.ActivationFunctionType
ALU = mybir.AluOpType

ENGS = ("pe", "act", "dve", "pool", "sp")

S = 2048
D = 1024
TT = 512
NT = S // TT
MEM = 256
DFF = 4096
INC = 10240
EPS = 1e-6
NEG = -30000.0


class Op:
    __slots__ = ("eng", "fn", "deps", "signal", "sval", "is_dma", "dsem", "dval")

    def __init__(self, eng, fn, is_dma):
        self.eng = eng
        self.fn = fn
        self.deps = []
        self.signal = False
        self.sval = 0
        self.is_dma = is_dma
        self.dsem = None
        self.dval = 0


class Prog:
    def __init__(self, n_dma_sems=None, self_sync=True):
        self.ops = {e: [] for e in ENGS}
        self.last_w = {}
        self.readers = {}
        self.n_dma_sems = n_dma_sems or {"sp": 8, "pool": 24, "act": 4}
        self.dma_count = {}
        self.dma_rr = {e: 0 for e in ENGS}
        self.dma_last = {}
        self.self_sync = self_sync

    def op(self, eng, fn, reads=(), writes=(), dma=False):
        o = Op(eng, fn, dma)
        deps = []
        for t in reads:
            w = self.last_w.get(t)
            if w is not None:
                deps.append(w)
        for t in writes:
            w = self.last_w.get(t)
            if w is not None:
                deps.append(w)
            deps.extend(self.readers.get(t, ()))
        if dma:
            k = self.dma_rr[eng] % self.n_dma_sems[eng]
            self.dma_rr[eng] += 1
            key = (eng, k)
            self.dma_count[key] = self.dma_count.get(key, 0) + 1
            o.dsem = key
            o.dval = 16 * self.dma_count[key]
            prev = self.dma_last.get(key)
            if prev is not None:
                deps.append(prev)
            self.dma_last[key] = o
        seen = set()
        for d in deps:
            if id(d) in seen:
                continue
            seen.add(id(d))
            if (not d.is_dma) and d.eng == eng:
                if eng == "pe" or not self.self_sync:
                    continue
            o.deps.append(d)
            if not d.is_dma:
                d.signal = True
        for t in reads:
            self.readers.setdefault(t, []).append(o)
        for t in writes:
            self.last_w[t] = o
            self.readers[t] = []
        self.ops[eng].append(o)
        return o

    def emit(self, block_engines, sems, dma_sems, final_waits):
        for e in ENGS:
            c = 0
            for o in self.ops[e]:
                if o.signal and not o.is_dma:
                    c += 1
                    o.sval = c

        def dep_key(d):
            if d.is_dma:
                return ("d",) + d.dsem, d.dval
            return ("e", d.eng), d.sval

        def sem_of(k):
            return dma_sems[k[1:]] if k[0] == "d" else sems[k[1]]

        def run(e, eng):
            known = {}
            for o in self.ops[e]:
                need = {}
                for d in o.deps:
                    k, v = dep_key(d)
                    if v > need.get(k, 0):
                        need[k] = v
                for k, v in need.items():
                    if known.get(k, 0) >= v:
                        continue
                    known[k] = v
                    eng.wait_ge(sem_of(k), v)
                ins = o.fn(eng)
                if o.is_dma:
                    ins.then_inc(dma_sems[o.dsem], 16)
                elif o.signal:
                    ins.then_inc(sems[e], 1)
            for d in final_waits.get(e, ()):
                k, v = dep_key(d)
                eng.wait_ge(sem_of(k), v)

        for e in ENGS:
            block_engines[e](lambda eng, e=e: run(e, eng))


def build(ntiles=NT, debug=(), lvl=9, skip=()):
    nc = bass.Bass("TRN2", target_bir_lowering=False)

    def din(name, shape, dt=F32):
        return nc.dram_tensor(name, list(shape), dt, kind="ExternalInput").ap()

    x_d = din("x", [S, D])
    mem_d = din("mem", [MEM, D])
    w_in_d = din("w_in", [D, INC])
    w_co_d = din("w_conv_out", [D, D])
    w_so_d = din("w_sb_out", [D, D])
    w_xo_d = din("w_x_out", [D, D])
    w_o_d = din("w_out", [D, D])
    w_kv_d = din("w_mem_kv", [D, 2 * D])
    w_up_d = din("w_up", [D, DFF])
    w_dn_d = din("w_down", [DFF, D])
    vec_d = din("vec", [128, 52])
    cst_d = din("cst", [128, 5, 128], BF)
    out_d = nc.dram_tensor("out", [S, D], F32, kind="ExternalOutput").ap()

    def dint(name, shape):
        return nc.dram_tensor(name, list(shape), BF, kind="Internal").ap()

    b_in = dint("b_in", [D, INC])
    b_co = dint("b_co", [D, D])
    b_so = dint("b_so", [D, D])
    b_xo = dint("b_xo", [D, D])
    b_o = dint("b_o", [D, D])
    b_kv = dint("b_kv", [D, 2 * D])
    b_up = dint("b_up", [D, DFF])
    b_dn = dint("b_dn", [DFF, D])

    dbg_outs = {}
    P = Prog()
    with ExitStack() as es:
        def sb(name, shape, dt):
            return es.enter_context(nc.sbuf_tensor(name, list(shape), dt))

        xt = sb("xt", [128, 4, D], F32)
        hT = sb("hT", [128, 8, TT], BF)
        big = sb("big", [128, 32, TT], BF)
        osbT = sb("osbT", [128, 8, TT], BF)
        kT = sb("kT", [128, 8, S], BF)
        vA = sb("vA", [128, 16, D], BF)
        knT = sb("knT", [128, 8, MEM], BF)
        vmem = sb("vmem", [128, 2, D], BF)
        NRING = 4
        wring = [sb(f"ws{i}", [128, 8, 512], BF) for i in range(NRING)]
        junk = sb("junk", [128, D], BF)
        hb = [sb(f"hb{i}", [128, D], BF) for i in range(2)]
        NTF = 6
        tf = [sb(f"tf{i}", [128, 512], F32) for i in range(NTF)]
        NTB = 10
        tb = [sb(f"tb{i}", [128, 512], BF) for i in range(NTB)]
        ubuf = [sb(f"ub{i}", [128, 516], F32) for i in range(2)]
        ucar = sb("ucar", [128, 8, 2], F32)
        cst = sb("cst_sb", [128, 5, 128], BF)
        vec = sb("vecs", [128, 52], F32)
        gqk = sb("gqk", [128, 2], F32)
        NSTAT = 8
        stat = sb("stat", [128, NSTAT, 4], F32)
        banks = [es.enter_context(nc.psum_tensor(f"ps{i}", [128, 512], F32)) for i in range(8)]
        sems = {e: es.enter_context(nc.semaphore(f"s_{e}")) for e in ENGS}
        dsems = {}
        for e in ("sp", "pool", "act"):
            for k in range(P.n_dma_sems[e]):
                dsems[(e, k)] = es.enter_context(nc.semaphore(f"d_{e}{k}"))
        block = es.enter_context(nc.Block())

        ident = cst[:, 0, :]
        negU = cst[:, 1, :]
        negones = cst[:, 2, :]
        ones = cst[:, 3, :]
        negmask = cst[:, 4, :]
        gmix = vec[:, 0:8]
        gmem = vec[:, 8:16]
        gmlp = vec[:, 16:24]
        convw = vec[:, 24:48].rearrange("p (c k) -> p c k", k=3)
        qg = vec[:, 48:50]
        kg = vec[:, 50:52]

        qT = big[:, 0:8, :]
        convT = big[:, 8:16, :]
        xqnT = big[:, 16:24, :]
        mergedT = big[:, 16:24, :]
        oxT = big[:, 24:32, :]
        uT = big

        final_ops = []
        WPAIRS = [(b_in, w_in_d, "in"), (b_co, w_co_d, "co"), (b_so, w_so_d, "so"), (b_xo, w_xo_d, "xo"),
                  (b_o, w_o_d, "o"), (b_kv, w_kv_d, "kv"), (b_up, w_up_d, "up"), (b_dn, w_dn_d, "dn")]
        cast_done = set()

        state = {"bank": 0, "ring": 0, "stat": 0, "hb": 0, "tf": 0, "tb": 0, "ub": 0, "ev": 0}

        def nb():
            i = state["bank"] % 8
            state["bank"] += 1
            return i

        def ntf():
            i = state["tf"] % NTF
            state["tf"] += 1
            return i

        def ntb():
            i = state["tb"] % NTB
            state["tb"] += 1
            return i

        def dbg(name, ap, shape, reads):
            if name not in debug:
                return
            dt = ap.dtype
            t = nc.dram_tensor("dbg_" + name, list(shape), dt, kind="ExternalOutput").ap()
            dbg_outs[name] = t
            o = P.op("pool", C("dma_start", out=t, in_=ap), reads=reads, dma=True)
            final_ops.append(o)

        def C(meth, *a, **kw):
            return lambda e: getattr(e, meth)(*a, **kw)

        def act(fn, reads, writes):
            return P.op("act", fn, reads, writes)

        def dve(fn, reads, writes):
            return P.op("dve", fn, reads, writes)

        def pool(fn, reads, writes):
            return P.op("pool", fn, reads, writes)

        def mm(out, lhsT, rhs, start, stop, reads, writes, skip=False):
            return P.op("pe", C("matmul", out, lhsT=lhsT, rhs=rhs, start=start, stop=stop, skip_group_check=skip),
                        reads, writes)

        def evac(dst, src, reads, writes, scale=None):
            state["ev"] += 1
            if state["ev"] % 2 == 0:
                if scale is None:
                    act(C("activation", out=dst, in_=src, func=AF.Copy), reads, writes)
                else:
                    act(C("activation", out=dst, in_=src, func=AF.Copy, scale=scale), reads, writes)
            else:
                if scale is None:
                    dve(C("tensor_copy", out=dst, in_=src), reads, writes)
                else:
                    dve(C("tensor_scalar", out=dst, in0=src, scalar1=scale, scalar2=None, op0=ALU.mult),
                        reads, writes)

        def stream(wap, r0, c0, segtok, ncols=512):
            if ncols == 512:
                if state["ring"] % 2:
                    state["ring"] += 1
                h = state["ring"] % (2 * NRING)
                state["ring"] += 2
                dst = wring[h // 2][:, :, :]
                toks = [("ws", h), ("ws", h + 1)]
            else:
                h = state["ring"] % (2 * NRING)
                state["ring"] += 1
                dst = wring[h // 2][:, :, (h % 2) * 256:(h % 2) * 256 + 256]
                toks = [("ws", h)]
            src = wap[r0:r0 + 1024, c0:c0 + ncols].rearrange("(kc p) n -> p kc n", p=128)
            wname, wfp = [(n_, f_) for (b_, f_, n_) in WPAIRS if b_ is wap][0]
            btok = ("blk", wname, r0, c0, ncols)
            if btok not in cast_done:
                cast_done.add(btok)
                fsrc = wfp[r0:r0 + 1024, c0:c0 + ncols].rearrange("(kc p) n -> p kc n", p=128)
                P.op("pool", C("dma_start", out=dst, in_=fsrc), writes=toks, dma=True)
                P.op("sp", C("dma_start", out=src, in_=dst), reads=toks, writes=[btok], dma=True)
            else:
                P.op("sp", C("dma_start", out=dst, in_=src), reads=[btok], writes=toks, dma=True)
            return dst, toks

        def mm8_feat(bank, slot, slot_tok, cc, rhsT, rhs_toks, n=TT):
            for kc in range(8):
                mm(banks[bank][:, 0:n], slot[:, kc, cc * 128:(cc + 1) * 128], rhsT[:, kc, 0:n],
                   kc == 0, kc == 7, slot_tok + rhs_toks, [("ps", bank)])

        def mm8_tok(bank, lhsT_buf, lhs_toks, b, slot, slot_tok, n=512):
            for kc in range(8):
                mm(banks[bank][:, 0:n], lhsT_buf[:, kc, b * 128:(b + 1) * 128], slot[:, kc, 0:n],
                   kc == 0, kc == 7, slot_tok + lhs_toks, [("ps", bank)])

        HT_TOKS = [("hT", b) for b in range(4)]

        def rstd_small(src_ap, n_inv, reads):
            si = state["stat"] % NSTAT
            state["stat"] += 1
            ss = stat[:, si, 0:1]
            lv = stat[:, si, 1:2]
            rs = stat[:, si, 2:3]
            tk = ("stat", si)
            act(C("activation", out=junk[:], in_=src_ap, func=AF.Square, accum_out=ss),
                reads, ["junk", tk])
            act(C("activation", out=lv, in_=ss, func=AF.Ln, scale=n_inv, bias=EPS), [tk], [tk])
            act(C("activation", out=rs, in_=lv, func=AF.Exp, scale=-0.5), [tk], [tk])
            return rs, tk

        def norm_transpose(src_ap, src_toks, gv, dstT, col0, dst_tok):
            rs, tk = rstd_small(src_ap, 1.0 / D, src_toks)
            hi = state["hb"] % 2
            state["hb"] += 1
            h = hb[hi]
            act(C("activation", out=h[:], in_=src_ap, func=AF.Copy, scale=rs), src_toks + [tk], [("hb", hi)])
            bk = nb()
            pb = banks[bk][:].bitcast(BF)
            for c in range(8):
                P.op("pe", C("transpose", out=pb[:, c * 128:(c + 1) * 128],
                                                      in_=h[:, c * 128:(c + 1) * 128], identity=ident),
                     [("hb", hi), "cst"], [("ps", bk)])
            dve(C("tensor_tensor", out=dstT[:, :, col0:col0 + 128],
                                          in0=pb.rearrange("p (c t) -> p c t", c=8),
                                          in1=gv.unsqueeze(2).to_broadcast([128, 8, 128]), op=ALU.mult),
                [("ps", bk), "vec"], [dst_tok])

        def rstd_bcast(bankC, n_inv, n):
            ti = ntf()
            t = tf[ti]
            act(C("activation", out=t[:, 0:n], in_=banks[bankC][:, 0:n], func=AF.Ln, scale=n_inv, bias=EPS),
                [("ps", bankC)], [("tf", ti)])
            act(C("activation", out=t[:, 0:n], in_=t[:, 0:n], func=AF.Exp, scale=-0.5),
                [("tf", ti)], [("tf", ti)])
            return t, ("tf", ti)

        P.op("sp", C("dma_start", out=cst[:], in_=cst_d), writes=["cst"], dma=True)
        P.op("sp", C("dma_start", out=vec[:], in_=vec_d), writes=["vec"], dma=True)
        for mb in range(2):
            P.op("sp", C("dma_start", out=xt[:, mb, :], in_=mem_d[mb * 128:(mb + 1) * 128, :]),
                 writes=[("xt", mb)], dma=True)

        pool(C("memset", ucar[:], 0.0), [], ["ucar"])
        dve(C("scalar_tensor_tensor", out=gqk[:], in0=qg, scalar=1.0 / 16.0, in1=kg,
                                             op0=ALU.mult, op1=ALU.mult), ["vec"], ["gqk"])

        memT = osbT[:, :, 0:MEM]
        OSB_TOKS = [("osb", c) for c in range(8)]
        for mb in range(2 if lvl >= 1 else 0):
            norm_transpose(xt[:, mb, :], [("xt", mb)], gmem, memT, mb * 128, ("memT", mb))
        MEMT_TOKS = [("memT", 0), ("memT", 1)] + OSB_TOKS
        for blk in range(2 if lvl >= 1 else 0):
            slot, stok = stream(b_kv, 0, blk * 512, ("w", "kv", 0))
            for hl in range(2):
                hx = blk * 2 + hl
                bk = [nb(), nb()]
                sq = []
                for dc in range(2):
                    mm8_feat(bk[dc], slot, stok, hl * 2 + dc, memT, MEMT_TOKS, n=MEM)
                    ti = ntb()
                    act(C("activation", out=tb[ti][:, 0:MEM], in_=banks[bk[dc]][:, 0:MEM],
                                                             func=AF.Square), [("ps", bk[dc])], [("tb", ti)])
                    sq.append(ti)
                bC = nb()
                for dc in range(2):
                    mm(banks[bC][:, 0:MEM], ones, tb[sq[dc]][:, 0:MEM], dc == 0, dc == 1,
                       ["cst", ("tb", sq[dc])], [("ps", bC)])
                rq, rtok = rstd_bcast(bC, 1.0 / 256, MEM)
                for dc in range(2):
                    dve(C("scalar_tensor_tensor", out=knT[:, 2 * hx + dc, :], in0=banks[bk[dc]][:, 0:MEM],
                                                                scalar=gqk[:, dc:dc + 1], in1=rq[:, 0:MEM],
                                                                op0=ALU.mult, op1=ALU.mult),
                        [("ps", bk[dc]), rtok, "gqk"], [("knT", 2 * hx + dc)])
        for half in range(2 if lvl >= 1 else 0):
            slot, stok = stream(b_kv, 0, 1024 + half * 512, ("w", "kv", 0))
            for mc in range(2):
                bk = nb()
                mm8_tok(bk, memT, MEMT_TOKS, mc, slot, stok)
                evac(vmem[:, mc, half * 512:(half + 1) * 512], banks[bk][:], [("ps", bk)], [("vmem", mc, half)])
        KN_TOKS = [("knT", c) for c in range(8)]
        VM_TOKS = [("vmem", mc, h) for mc in range(2) for h in range(2)]
        dbg("knT", knT[:], [128, 8, MEM], KN_TOKS)
        dbg("vmem", vmem[:], [128, 2, D], VM_TOKS)

        for T in range(ntiles):
            t0 = T * TT
            for b in range(4):
                P.op("act", C("dma_start", out=xt[:, b, :], in_=x_d[t0 + b * 128:t0 + (b + 1) * 128, :]),
                     writes=[("xt", b)], dma=True)
            for b in range(4 if lvl >= 2 else 0):
                norm_transpose(xt[:, b, :], [("xt", b)], gmix, hT, b * 128, ("hT", b))
            if T == 0:
                dbg("hT", hT[:], [128, 8, TT], HT_TOKS)

            for blk in range(2 if lvl >= 2 else 0):
                slot, stok = stream(b_in, 0, 3072 + blk * 512, ("w", "in", 1))
                for cc in range(4):
                    c = blk * 4 + cc
                    bk = nb()
                    mm8_feat(bk, slot, stok, cc, hT, HT_TOKS)
                    evac(qT[:, c, :], banks[bk][:], [("ps", bk)], [("big", c)], scale=0.125)
            for blk in range(2 if lvl >= 2 else 0):
                slot, stok = stream(b_in, 0, 4096 + blk * 512, ("w", "in", 2))
                for cc in range(4):
                    c = blk * 4 + cc
                    bk = nb()
                    mm8_feat(bk, slot, stok, cc, hT, HT_TOKS)
                    evac(kT[:, c, t0:t0 + TT], banks[bk][:], [("ps", bk)], [("kT", c, T)])
            for half in range(2 if lvl >= 2 else 0):
                slot, stok = stream(b_in, 0, 5120 + half * 512, ("w", "in", 2))
                for b in range(4):
                    bk = nb()
                    mm8_tok(bk, hT, HT_TOKS, b, slot, stok)
                    evac(vA[:, T * 4 + b, half * 512:(half + 1) * 512], banks[bk][:], [("ps", bk)],
                         [("vA", T * 4 + b, half)])
            if T == 0:
                dbg("qT", qT, [128, 8, TT], [("big", c) for c in range(8)])
                dbg("kT0", kT[:, :, 0:TT], [128, 8, TT], [("kT", c, 0) for c in range(8)])
                dbg("v0", vA[:, 0:4, :], [128, 4, D], [("vA", b, h) for b in range(4) for h in range(2)])

            for j in range(2 if lvl >= 3 else 0):
                sl_h, tk_h = stream(b_in, 0, j * 512, ("w", "in", 0))
                sl_c, tk_c = stream(b_in, 0, 2048 + j * 512, ("w", "in", 1))
                sl_b, tk_b = stream(b_in, 0, 1024 + j * 512, ("w", "in", 0))
                for cc in range(4):
                    c = j * 4 + cc
                    bh = nb()
                    mm8_feat(bh, sl_h, tk_h, cc, hT, HT_TOKS)
                    thi = ntf()
                    act(C("activation", out=tf[thi][:], in_=banks[bh][:], func=AF.Copy),
                        [("ps", bh)], [("tf", thi)])
                    bc = nb()
                    mm8_feat(bc, sl_c, tk_c, cc, hT, HT_TOKS)
                    ui = state["ub"] % 2
                    state["ub"] += 1
                    ub = ubuf[ui]
                    utok = ("ub", ui)
                    pool(C("tensor_copy", out=ub[:, 0:2], in_=ucar[:, c, :]), ["ucar"], [utok])
                    dve(C("tensor_tensor", out=ub[:, 2:514], in0=tf[thi][:],
                                                                         in1=banks[bc][:], op=ALU.mult),
                        [("tf", thi), ("ps", bc)], [utok])
                    pool(C("tensor_copy", out=ucar[:, c, :], in_=ub[:, 512:514]), [utok], ["ucar"])
                    ai = ntf()
                    acc = tf[ai]
                    atok = ("tf", ai)
                    dve(C("tensor_scalar", out=acc[:], in0=ub[:, 2:514],
                                                                       scalar1=convw[:, c, 2:3], scalar2=None,
                                                                       op0=ALU.mult), [utok, "vec"], [atok])
                    dve(C("scalar_tensor_tensor", out=acc[:], in0=ub[:, 1:513],
                                                                              scalar=convw[:, c, 1:2], in1=acc[:],
                                                                              op0=ALU.mult, op1=ALU.add),
                        [utok, "vec", atok], [atok])
                    dve(C("scalar_tensor_tensor", out=acc[:], in0=ub[:, 0:512],
                                                                              scalar=convw[:, c, 0:1], in1=acc[:],
                                                                              op0=ALU.mult, op1=ALU.add),
                        [utok, "vec", atok], [atok])
                    bb = nb()
                    mm8_feat(bb, sl_b, tk_b, cc, hT, HT_TOKS)
                    dve(C("tensor_tensor", out=convT[:, c, :], in0=acc[:], in1=banks[bb][:],
                                                                       op=ALU.mult),
                        [atok, ("ps", bb)], [("big", 8 + c)])
            if T == 0:
                dbg("convT", convT, [128, 8, TT], [("big", 8 + c) for c in range(8)])

            for blk in range(2 if lvl >= 4 else 0):
                slot, stok = stream(b_in, 0, 6144 + blk * 512, ("w", "in", 3))
                for hl in range(2):
                    hx = blk * 2 + hl
                    bk = [nb(), nb()]
                    sq = []
                    for dc in range(2):
                        mm8_feat(bk[dc], slot, stok, hl * 2 + dc, hT, HT_TOKS)
                        ti = ntb()
                        act(C("activation", out=tb[ti][:], in_=banks[bk[dc]][:],
                                                                        func=AF.Square), [("ps", bk[dc])], [("tb", ti)])
                        sq.append(ti)
                    bC = nb()
                    for dc in range(2):
                        mm(banks[bC][:], ones, tb[sq[dc]][:], dc == 0, dc == 1, ["cst", ("tb", sq[dc])], [("ps", bC)])
                    rq, rtok = rstd_bcast(bC, 1.0 / 256, TT)
                    for dc in range(2):
                        dve(C("tensor_tensor", out=xqnT[:, 2 * hx + dc, :],
                                                                                  in0=rq[:], in1=banks[bk[dc]][:],
                                                                                  op=ALU.mult),
                            [("ps", bk[dc]), rtok], [("big", 16 + 2 * hx + dc)])
            if T == 0:
                dbg("xqnT", xqnT, [128, 8, TT], [("big", 16 + c) for c in range(8)])
            for hx in range(4 if lvl >= 4 else 0):
                pts = []
                for mc in range(2):
                    bS = nb()
                    for dc in range(2):
                        mm(banks[bS][:], knT[:, 2 * hx + dc, mc * 128:(mc + 1) * 128], xqnT[:, 2 * hx + dc, :],
                           dc == 0, dc == 1, KN_TOKS + [("big", 16 + 2 * hx + dc)], [("ps", bS)])
                    ti = ntb()
                    act(C("activation", out=tb[ti][:], in_=banks[bS][:], func=AF.Exp),
                        [("ps", bS)], [("tb", ti)])
                    pts.append(ti)
                bD = nb()
                for mc in range(2):
                    mm(banks[bD][:], ones, tb[pts[mc]][:], mc == 0, mc == 1, ["cst", ("tb", pts[mc])], [("ps", bD)])
                ri = ntf()
                dve(C("reciprocal", out=tf[ri][:], in_=banks[bD][:]), [("ps", bD)], [("tf", ri)])
                for dc in range(2):
                    bO = nb()
                    for mc in range(2):
                        mm(banks[bO][:], vmem[:, mc, hx * 256 + dc * 128: hx * 256 + (dc + 1) * 128], tb[pts[mc]][:],
                           mc == 0, mc == 1, VM_TOKS + [("tb", pts[mc])], [("ps", bO)])
                    dve(C("tensor_tensor", out=oxT[:, 2 * hx + dc, :], in0=tf[ri][:],
                                                                              in1=banks[bO][:], op=ALU.mult),
                        [("tf", ri), ("ps", bO)], [("big", 24 + 2 * hx + dc)])
            if T == 0:
                dbg("oxT", oxT, [128, 8, TT], [("big", 24 + c) for c in range(8)])

            nk = 4 * T + 4
            units = []
            for pr in range(8 if (lvl >= 5 and "sb" not in skip) else 0):
                for kb in range(nk - 1, -1, -1):
                    for hh in range(2):
                        units.append({"pr": pr, "hh": hh, "kb": kb, "i": len(units)})
            ZB = [0, 1, 2, 3, 4, 5]
            OB = [6, 7]
            EB = [0, 1, 2, 3]
            SPB = [0, 1, 2, 3]
            AB = [4, 5, 6, 7]
            RB = [8, 9]

            def kv_toks(u):
                pr, kb = u["pr"], u["kb"]
                return ("kT", pr, kb // 4), [("vA", kb, (2 * pr) // 8)]

            def s0(u):
                pr, hh, kb, i = u["pr"], u["hh"], u["kb"], u["i"]
                c0 = max(0, kb - 4 * T) * 128
                u["c0"] = c0
                zb = ZB[i % len(ZB)]
                u["zb"] = zb
                diag = kb >= 4 * T
                p0 = 64 * hh
                ktok, _ = kv_toks(u)
                mm(banks[zb][:, c0:512], kT[p0:p0 + 64, pr, kb * 128:(kb + 1) * 128], qT[p0:p0 + 64, pr, c0:512],
                   True, True, [ktok, ("big", pr)], [("ps", zb)])
                if diag:
                    mm(banks[zb][:, c0:c0 + 128], ident, negmask, False, True, ["cst"], [("ps", zb)], skip=True)

            def s1(u):
                i, c0, zb = u["i"], u["c0"], u["zb"]
                eb = EB[i % len(EB)]
                u["eb"] = eb
                act(C("activation", out=tf[eb][:, c0:512], in_=banks[zb][:, c0:512], func=AF.Exp),
                    [("ps", zb)], [("tf", eb)])

            def s2(u):
                i, c0, eb = u["i"], u["c0"], u["eb"]
                sp = SPB[i % len(SPB)]
                u["sp"] = sp
                act(C("activation", out=tb[sp][:, c0:512], in_=tf[eb][:, c0:512], func=AF.Ln, bias=1.0, scale=1.0),
                    [("tf", eb)], [("tb", sp)])

            def s3(u):
                hh, kb, c0, zb, sp = u["hh"], u["kb"], u["c0"], u["zb"], u["sp"]
                rb = RB[hh]
                has_r = kb < nk - 1
                mm(banks[zb][:, c0:512], negU, tb[sp][:, c0:512], False, not has_r, ["cst", ("tb", sp)], [("ps", zb)],
                   skip=True)
                if has_r:
                    mm(banks[zb][:, c0:512], negones, tb[rb][:, c0:512], False, True, ["cst", ("tb", rb)], [("ps", zb)],
                       skip=True)
                if kb == nk - 1:
                    pool(C("memset", tb[rb][:], 0.0), [], [("tb", rb)])
                if kb > 0:
                    dve(C("tensor_tensor", out=tb[rb][:, c0:512], in0=tb[rb][:, c0:512], in1=tb[sp][:, c0:512],
                                                   op=ALU.add), [("tb", rb), ("tb", sp)], [("tb", rb)])

            def s4(u):
                i, c0, zb = u["i"], u["c0"], u["zb"]
                ab = AB[i % len(AB)]
                u["ab"] = ab
                act(C("activation", out=tb[ab][:, c0:512], in_=banks[zb][:, c0:512], func=AF.Exp),
                    [("ps", zb)], [("tb", ab)])

            def s5(u):
                pr, hh, kb, c0, ab = u["pr"], u["hh"], u["kb"], u["c0"], u["ab"]
                ob = OB[pr % 2]
                hd = 2 * pr + hh
                _, vtoks = kv_toks(u)
                mm(banks[ob][64 * hh:64 * hh + 64, c0:512], vA[:, kb, hd * 64:(hd + 1) * 64], tb[ab][:, c0:512],
                   kb == nk - 1, kb == 0, vtoks + [("tb", ab)], [("ps", ob)], skip=True)
                if kb == 0 and hh == 1:
                    dve(C("tensor_copy", out=osbT[:, pr, :], in_=banks[ob][:]), [("ps", ob)], [("osb", pr)])

            npairs = len(units) // 2
            for it in range(npairs + 2):
                for grp, lag in (((s0, s1, s2), 0), ((s3, s4), 1), ((s5,), 2)):
                    pi = it - lag
                    if 0 <= pi < npairs:
                        for st in grp:
                            for u in (units[2 * pi], units[2 * pi + 1]):
                                st(u)
            if T == 0:
                dbg("osbT", osbT[:], [128, 8, TT], OSB_TOKS)

            BR = [(7168, b_co, ("w", "co", 0), convT, [("big", 8 + c) for c in range(8)], 3),
                  (8192, b_so, ("w", "so", 0), osbT, OSB_TOKS, 4),
                  (9216, b_xo, ("w", "xo", 0), oxT, [("big", 24 + c) for c in range(8)], 4)]
            for g4 in range(4 if lvl >= 6 else 0):
                slots = []
                for (gcol, wy, wtok, _, _, gseg) in BR:
                    sg = stream(b_in, 0, gcol + g4 * 256, ("w", "in", gseg), ncols=256)
                    sy = stream(wy, 0, g4 * 256, wtok, ncols=256)
                    slots.append((sg, sy))
                for cc in range(2):
                    c = g4 * 2 + cc
                    gts = []
                    for br in range(3):
                        (sg, sgt), _ = slots[br]
                        bk = nb()
                        mm8_feat(bk, sg, sgt, cc, hT, HT_TOKS)
                        gi = ntf()
                        act(C("activation", out=tf[gi][:], in_=banks[bk][:], func=AF.Tanh, scale=0.5),
                            [("ps", bk)], [("tf", gi)])
                        gts.append(gi)
                    prev = None
                    for br in range(3):
                        _, (sy, syt) = slots[br]
                        _, _, _, inT, in_toks, _ = BR[br]
                        bk = nb()
                        mm8_feat(bk, sy, syt, cc, inT, in_toks)
                        gi = gts[br]
                        dve(C("scalar_tensor_tensor", out=tf[gi][:], in0=tf[gi][:], scalar=1.0, in1=banks[bk][:],
                              op0=ALU.add, op1=ALU.mult), [("tf", gi), ("ps", bk)], [("tf", gi)])
                        if br == 1:
                            pool(C("tensor_tensor", out=tf[gts[0]][:], in0=tf[gts[0]][:], in1=tf[gi][:], op=ALU.add),
                                 [("tf", gts[0]), ("tf", gi)], [("tf", gts[0])])
                        if br == 2:
                            pool(C("tensor_tensor", out=mergedT[:, c, :], in0=tf[gts[0]][:], in1=tf[gi][:], op=ALU.add),
                                 [("tf", gts[0]), ("tf", gi)], [("big", 16 + c)])
            MG_TOKS = [("big", 16 + c) for c in range(8)]
            if T == 0:
                dbg("mergedT", mergedT, [128, 8, TT], MG_TOKS)

            for half in range(2 if lvl >= 7 else 0):
                slot, stok = stream(b_o, 0, half * 512, ("w", "o", 0))
                for b in range(4):
                    bk = nb()
                    mm8_tok(bk, mergedT, MG_TOKS, b, slot, stok)
                    dve(C("scalar_tensor_tensor", out=xt[:, b, half * 512:(half + 1) * 512], in0=banks[bk][:], scalar=0.5,
                          in1=xt[:, b, half * 512:(half + 1) * 512], op0=ALU.mult, op1=ALU.add),
                        [("xt", b), ("ps", bk)], [("xt", b)])
            if T == 0:
                dbg("x1", xt[:], [128, 4, D], [("xt", b) for b in range(4)])

            for b in range(4 if lvl >= 8 else 0):
                norm_transpose(xt[:, b, :], [("xt", b)], gmlp, hT, b * 128, ("hT", b))
            for ublk in range(8 if lvl >= 8 else 0):
                slot, stok = stream(b_up, 0, ublk * 512, ("w", "up", ublk // 4))
                for cc in range(4):
                    j = ublk * 4 + cc
                    bk = nb()
                    mm8_feat(bk, slot, stok, cc, hT, HT_TOKS)
                    ri = ntf()
                    act(C("activation", out=tf[ri][:], in_=banks[bk][:], func=AF.Relu),
                        [("ps", bk)], [("tf", ri)])
                    dve(C("tensor_tensor", out=uT[:, j, :], in0=tf[ri][:], in1=banks[bk][:], op=ALU.mult),
                        [("tf", ri), ("ps", bk)], [("big", j)])
            for half in range(2 if lvl >= 9 else 0):
                accb = [4 * half + b for b in range(4)]
                for rg in range(4):
                    slot, stok = stream(b_dn, rg * 1024, half * 512, ("w", "dn", rg))
                    for b in range(4):
                        for kc in range(8):
                            j = rg * 8 + kc
                            mm(banks[accb[b]][:], uT[:, j, b * 128:(b + 1) * 128], slot[:, kc, :],
                               rg == 0 and kc == 0, rg == 3 and kc == 7, stok + [("big", j)], [("ps", accb[b])])
                for b in range(4):
                    dve(C("tensor_tensor", out=xt[:, b, half * 512:(half + 1) * 512],
                          in0=xt[:, b, half * 512:(half + 1) * 512], in1=banks[accb[b]][:], op=ALU.add),
                        [("xt", b)] + [("ps", accb[bb]) for bb in range(4)], [("xt", b)])
            for b in range(4):
                o = P.op("pool", C("dma_start", out=out_d[t0 + b * 128:t0 + (b + 1) * 128, :], in_=xt[:, b, :]),
                         reads=[("xt", b)], dma=True)
                final_ops.append(o)

        be = {"pe": block.tensor, "act": block.scalar, "dve": block.vector, "pool": block.gpsimd, "sp": block.sync}
        P.emit(be, sems, dsems, {"pool": final_ops})
    return nc, dbg_outs


def _consts():
    j = np.arange(128)[:, None]
    s = np.arange(128)[None, :]
    c = np.zeros((128, 5, 128), np.float32)
    c[:, 0, :] = np.eye(128)
    c[:, 1, :] = -1.0 * (j >= s)
    c[:, 2, :] = -1.0
    c[:, 3, :] = 1.0
    c[:, 4, :] = NEG * (j >= s)
    return c.astype(ml_dtypes.bfloat16)


def _vecs(g_mix, g_mem, g_mlp, conv_w, q_norm_g, k_norm_g):
    v = np.zeros((128, 52), np.float32)
    v[:, 0:8] = g_mix.reshape(8, 128).T
    v[:, 8:16] = g_mem.reshape(8, 128).T
    v[:, 16:24] = g_mlp.reshape(8, 128).T
    v[:, 24:48] = conv_w.reshape(3, 8, 128).transpose(2, 1, 0).reshape(128, 24)
    v[:, 48:50] = q_norm_g.reshape(2, 128).T
    v[:, 50:52] = k_norm_g.reshape(2, 128).T
    return v


_NC_CACHE = {}


def kernel(x, mem, g_mix, g_mem, w_in, conv_w, w_conv_out, w_sb_out, q_norm_g, k_norm_g,
           w_mem_kv, w_x_out, w_out, g_mlp, w_up, w_down):
    f = lambda a: np.ascontiguousarray(np.asarray(a, dtype=np.float32))
    if "nc" not in _NC_CACHE:
        _NC_CACHE["nc"] = build()[0]
    nc = _NC_CACHE["nc"]
    shared = {
        "w_in": f(w_in[0]), "w_conv_out": f(w_conv_out[0]), "w_sb_out": f(w_sb_out[0]),
        "w_x_out": f(w_x_out[0]), "w_out": f(w_out[0]), "w_mem_kv": f(w_mem_kv[0]),
        "w_up": f(w_up[0]), "w_down": f(w_down[0]),
        "vec": _vecs(f(g_mix[0]), f(g_mem[0]), f(g_mlp[0]), f(conv_w[0]), f(q_norm_g[0]), f(k_norm_g[0])),
        "cst": _consts(),
    }
    x = f(x)
    mem = f(mem)
    in_maps = []
    for b in range(8):
        m = dict(shared)
        m["x"] = x[b]
        m["mem"] = mem[b]
        in_maps.append(m)
    res = run_bass_kernel_spmd(nc, in_maps, core_ids=list(range(8)))
    return np.stack([np.asarray(r["out"], dtype=np.float32) for r in res.results], axis=0)
```
